# Optimizing a Trainium2 kernel written in Bass

```python
import math
import jax
import jax.numpy as jnp
from jax import lax
import numpy as np

D_MODEL = 1024
BATCH = 4
SEQ = 4096
DEPTH = 2

GRID_W = 64
CTX_LEN = 256
EPS = 1e-6

MLA_HEADS = 8
Q_LORA = 384
KV_LORA = 256
QK_NOPE = 64
QK_ROPE = 32
V_HEAD = 64
ROPE_BASE = 10000.0
Q_BLOCK = 128
SM_SCALE = (QK_NOPE + QK_ROPE) ** -0.5

LRU_WIDTH = 512
LRU_BLOCKS = 8
LRU_BLOCK = LRU_WIDTH // LRU_BLOCKS
LRU_CONV = 4
LRU_C = 8.0

HY_WIDTH = 512
HY_ORDER = 2
HY_SHORT = 3
HY_EMB = 33
HY_HID = 64
HY_INNER = 2
HY_FAST_DECAY = 0.3
HY_SLOW_DECAY = 1.5
HY_DECAY_TARGET = 1e-2

FFN_HID = ((8 * D_MODEL + 3 * 256 - 1) // (3 * 256)) * 256

N_BRANCH = 3
MLA_IN = Q_LORA + KV_LORA + QK_ROPE
IN_SPLITS = (MLA_IN, MLA_IN + LRU_WIDTH, MLA_IN + 2 * LRU_WIDTH, MLA_IN + 2 * LRU_WIDTH + 3 * HY_WIDTH)
IN_WIDTH = IN_SPLITS[-1] + N_BRANCH * D_MODEL

kernel_name = 'hybrid_mla_rglru_hyena_diffusion_trunk'

F32 = jnp.float32


def rmsnorm(x, g):
    xf = x.astype(F32)
    y = xf * lax.rsqrt(jnp.mean(xf * xf, axis=-1, keepdims=True) + EPS)
    return (y * g.astype(F32)).astype(x.dtype)


def modulate(h, shift, scale):
    return h * (1 + scale) + shift


def ada_mods(cond, w, b):
    m = (jax.nn.silu(cond) @ w + b)[..., None, :]
    return jnp.split(m, 6, axis=-1)


def short_conv(u, w, b):
    k = w.shape[0]
    left = k // 2
    y = lax.conv_general_dilated(u, w[:, None, :].astype(u.dtype), window_strides=(1,),
                                 padding=[(left, k - 1 - left)],
                                 dimension_numbers=('NWC', 'WIO', 'NWC'),
                                 feature_group_count=u.shape[-1])
    return y + b


def axial_rope(x):
    n = x.shape[1]
    rows = n // GRID_W
    row = jnp.repeat(jnp.arange(rows, dtype=F32), GRID_W)
    col = jnp.tile(jnp.arange(GRID_W, dtype=F32), rows)
    seg = QK_ROPE // 2
    inv = 1.0 / (ROPE_BASE ** (jnp.arange(seg // 2, dtype=F32) * 2.0 / seg))

    def rot(xs, pos):
        ang = pos[:, None] * inv
        cos = jnp.cos(ang)[:, None, :]
        sin = jnp.sin(ang)[:, None, :]
        x1, x2 = jnp.split(xs, 2, axis=-1)
        return jnp.concatenate([x1 * cos - x2 * sin, x1 * sin + x2 * cos], axis=-1)

    out = jnp.concatenate([rot(x[..., :seg], row), rot(x[..., seg:], col)], axis=-1)
    return out.astype(x.dtype)


def mla_qkv(z, q_g, w_uq, kv_g, w_ukv, with_pos):
    b, n, _ = z.shape
    cq, ckv, kpe = jnp.split(z, [Q_LORA, Q_LORA + KV_LORA], axis=-1)
    q = (rmsnorm(cq, q_g) @ w_uq).reshape(b, n, MLA_HEADS, QK_NOPE + QK_ROPE)
    kv = (rmsnorm(ckv, kv_g) @ w_ukv).reshape(b, n, MLA_HEADS, QK_NOPE + V_HEAD)
    q_nope, q_pe = jnp.split(q, [QK_NOPE], axis=-1)
    k_nope, v = jnp.split(kv, [QK_NOPE], axis=-1)
    k_pe = kpe[:, :, None, :]
    if with_pos:
        q_pe = axial_rope(q_pe)
        k_pe = axial_rope(k_pe)
    k_pe = jnp.broadcast_to(k_pe, (b, n, MLA_HEADS, QK_ROPE))
    q = jnp.concatenate([q_nope, q_pe], axis=-1)
    k = jnp.concatenate([k_nope, k_pe], axis=-1)
    return q, k, v


def softmax_attend(q, k, v):
    s = jnp.einsum('bqhd,bkhd->bhqk', q, k).astype(F32) * SM_SCALE
    p = jax.nn.softmax(s, axis=-1).astype(v.dtype)
    return jnp.einsum('bhqk,bkhd->bqhd', p, v)


def blocked_attend(q, k, v):
    b, n, h, d = q.shape
    nb = n // Q_BLOCK
    qb = jnp.moveaxis(q.reshape(b, nb, Q_BLOCK, h, d), 1, 0)
    o = lax.map(lambda blk: softmax_attend(blk, k, v), qb)
    return jnp.moveaxis(o, 0, 1).reshape(b, n, h * v.shape[-1])


def block_diag_linear(u, w, bias):
    b, n, c = u.shape
    y = jnp.einsum('bngi,gij->bngj', u.reshape(b, n, LRU_BLOCKS, LRU_BLOCK), w)
    return y.reshape(b, n, c) + bias


def rglru_coeffs(u, w_a, b_a, w_x, b_x, lam):
    r = jax.nn.sigmoid(block_diag_linear(u, w_a, b_a).astype(F32))
    i = jax.nn.sigmoid(block_diag_linear(u, w_x, b_x).astype(F32))
    log_a = -LRU_C * r * jax.nn.softplus(-lam.astype(F32))
    a = jnp.exp(log_a)
    gain = jnp.sqrt(-jnp.expm1(2.0 * log_a))
    return a, gain * (i * u.astype(F32))


def _affine_combine(first, second):
    a1, b1 = first
    a2, b2 = second
    return a1 * a2, a2 * b1 + b2


def scan_forward(a, b, h0):
    b = b.at[:, 0].add(a[:, 0] * h0)
    return lax.associative_scan(_affine_combine, (a, b), axis=1)[1]


def scan_backward(a, b, h0):
    return jnp.flip(scan_forward(jnp.flip(a, 1), jnp.flip(b, 1), h0), 1)


def hyena_filters(n, w1, b1, w2, b2, freq, w_out):
    t = jnp.linspace(0.0, 1.0, n, dtype=F32)[:, None]
    bands = (HY_EMB - 1) // 2
    w = 2.0 * math.pi * jnp.arange(n, dtype=F32) / n
    f = jnp.linspace(1e-4, bands - 1, bands, dtype=F32)
    ang = w[:, None] * f[None, :]
    z = jnp.concatenate([t, jnp.cos(ang), -jnp.sin(ang)], axis=-1)
    fr = freq.astype(F32)
    h = jnp.sin(fr * (z @ w1.astype(F32) + b1.astype(F32)))
    for j in range(HY_INNER):
        h = jnp.sin(fr * (h @ w2[j].astype(F32) + b2[j].astype(F32)))
    h = (h @ w_out.astype(F32)).reshape(n, 2, HY_ORDER, HY_WIDTH)
    max_decay = math.log(HY_DECAY_TARGET) / HY_FAST_DECAY
    min_decay = math.log(HY_DECAY_TARGET) / HY_SLOW_DECAY
    deltas = jnp.abs(jnp.linspace(min_decay, max_decay, HY_WIDTH, dtype=F32))
    return h * jnp.exp(-t * deltas)[:, None, None, :]


def bidir_long_conv(u, h_fwd, h_bwd, skip):
    n = u.shape[1]
    k = jnp.concatenate([h_fwd[:1] + h_bwd[:1], h_fwd[1:], jnp.zeros_like(h_fwd[:1]), h_bwd[:0:-1]], axis=0)
    kf = jnp.fft.rfft(k, axis=0)
    uf = jnp.fft.rfft(u.astype(F32), n=2 * n, axis=1)
    y = jnp.fft.irfft(uf * kf[None], n=2 * n, axis=1)[:, :n]
    return y + u.astype(F32) * skip.astype(F32)


def hyena_mix(z, conv_w, conv_b, filt, skip):
    z = short_conv(z, conv_w, conv_b)
    v, x1, x2 = jnp.split(z, 3, axis=-1)
    y = x1.astype(F32) * bidir_long_conv(v, filt[:, 0, 0], filt[:, 1, 0], skip[0])
    y = x2.astype(F32) * bidir_long_conv(y, filt[:, 0, 1], filt[:, 1, 1], skip[1])
    return y.astype(z.dtype)


def merge_branches(gate_logits, ya, yb, yc, p):
    g = jax.nn.sigmoid(gate_logits.astype(F32)).astype(ya.dtype)
    ga, gb, gc = jnp.split(g, N_BRANCH, axis=-1)
    y = ga * (ya @ p['w_br_a']) + gb * (yb @ p['w_br_b']) + gc * (yc @ p['w_br_c'])
    return y @ p['w_out']


def swiglu(h, p):
    return (jax.nn.silu(h @ p['ffn_w_gate']) * (h @ p['ffn_w_up'])) @ p['ffn_w_down']


def trunk_layer(x, xc, c, c_ctx, p, ctx_out):
    sh1, sc1, g1, sh2, sc2, g2 = ada_mods(c, p['ada_w'], p['ada_b'])
    csh1, csc1, cg1, csh2, csc2, cg2 = ada_mods(c_ctx, p['ada_w'], p['ada_b'])
    z_lat = modulate(rmsnorm(x, p['norm1_g']), sh1, sc1) @ p['w_in']
    z_ctx = modulate(rmsnorm(xc, p['norm1_g']), csh1, csc1) @ p['w_in']
    mla_l, lx_l, lg_l, hy_l, gt_l = jnp.split(z_lat, IN_SPLITS, axis=-1)
    mla_c, lx_c, lg_c, hy_c, gt_c = jnp.split(z_ctx, IN_SPLITS, axis=-1)

    qc, kc, vc = mla_qkv(mla_c, p['q_norm_g'], p['w_uq'], p['kv_norm_g'], p['w_ukv'], False)
    ql, kl, vl = mla_qkv(mla_l, p['q_norm_g'], p['w_uq'], p['kv_norm_g'], p['w_ukv'], True)
    ya_l = blocked_attend(ql, jnp.concatenate([kc, kl], axis=1), jnp.concatenate([vc, vl], axis=1))

    uc = short_conv(lx_c, p['lru_conv_w'], p['lru_conv_b'])
    ul = short_conv(lx_l, p['lru_conv_w'], p['lru_conv_b'])
    fwd = (p['lru_wa'][0], p['lru_ba'][0], p['lru_wx'][0], p['lru_bx'][0], p['lru_lam'][0])
    bwd = (p['lru_wa'][1], p['lru_ba'][1], p['lru_wx'][1], p['lru_bx'][1], p['lru_lam'][1])
    h0 = jnp.zeros((xc.shape[0], LRU_WIDTH), F32)
    hcf = scan_forward(*rglru_coeffs(uc, *fwd), h0)
    hcb = scan_backward(*rglru_coeffs(uc, *bwd), h0)
    hlf = scan_forward(*rglru_coeffs(ul, *fwd), hcf[:, -1])
    hlb = scan_backward(*rglru_coeffs(ul, *bwd), hcb[:, 0])
    yb_l = (hlf + hlb).astype(lg_l.dtype) * jax.nn.gelu(lg_l)

    filt_args = (p['hy_w1'], p['hy_b1'], p['hy_w2'], p['hy_b2'], p['hy_freq'], p['hy_w_out'])
    yc_l = hyena_mix(hy_l, p['hy_conv_w'], p['hy_conv_b'], hyena_filters(x.shape[1], *filt_args), p['hy_skip'])

    x = x + g1 * merge_branches(gt_l, ya_l, yb_l, yc_l, p)
    x = x + g2 * swiglu(modulate(rmsnorm(x, p['norm2_g']), sh2, sc2), p)

    if ctx_out:
        ya_c = softmax_attend(qc, kc, vc).reshape(xc.shape[0], xc.shape[1], MLA_HEADS * V_HEAD)
        yb_c = (hcf + hcb).astype(lg_c.dtype) * jax.nn.gelu(lg_c)
        yc_c = hyena_mix(hy_c, p['hy_conv_w'], p['hy_conv_b'], hyena_filters(xc.shape[1], *filt_args), p['hy_skip'])
        xc = xc + cg1 * merge_branches(gt_c, ya_c, yb_c, yc_c, p)
        xc = xc + cg2 * swiglu(modulate(rmsnorm(xc, p['norm2_g']), csh2, csc2), p)
    return x, xc


def setup_inputs(seed: int = 0) -> dict:
    key = jax.random.key(seed)
    keys = iter(jax.random.split(key, 48))
    L, D = DEPTH, D_MODEL

    def nrm(shape, scale):
        return jax.random.normal(next(keys), shape, F32) * scale

    def gain(shape):
        return 1.0 + nrm(shape, 0.02)

    a_c = jax.random.uniform(next(keys), (L, 2, LRU_WIDTH), F32, 0.9, 0.999)
    a = a_c ** (1.0 / LRU_C)
    lam = jnp.log(a) - jnp.log1p(-a)
    return {
        'x': nrm((BATCH, SEQ, D), 1.0),
        'c': nrm((BATCH, D), 1.0),
        'ctx': nrm((BATCH, CTX_LEN, D), 1.0),
        'c_ctx': nrm((D,), 1.0),
        'ada_w': nrm((L, D, 6 * D), 0.5 * D ** -0.5),
        'ada_b': nrm((L, 6 * D), 0.02),
        'norm1_g': gain((L, D)),
        'norm2_g': gain((L, D)),
        'w_in': nrm((L, D, IN_WIDTH), D ** -0.5),
        'q_norm_g': gain((L, Q_LORA)),
        'w_uq': nrm((L, Q_LORA, MLA_HEADS * (QK_NOPE + QK_ROPE)), Q_LORA ** -0.5),
        'kv_norm_g': gain((L, KV_LORA)),
        'w_ukv': nrm((L, KV_LORA, MLA_HEADS * (QK_NOPE + V_HEAD)), KV_LORA ** -0.5),
        'lru_conv_w': nrm((L, LRU_CONV, LRU_WIDTH), LRU_CONV ** -0.5),
        'lru_conv_b': nrm((L, LRU_WIDTH), 0.02),
        'lru_wa': nrm((L, 2, LRU_BLOCKS, LRU_BLOCK, LRU_BLOCK), LRU_BLOCK ** -0.5),
        'lru_ba': nrm((L, 2, LRU_WIDTH), 0.02),
        'lru_wx': nrm((L, 2, LRU_BLOCKS, LRU_BLOCK, LRU_BLOCK), LRU_BLOCK ** -0.5),
        'lru_bx': nrm((L, 2, LRU_WIDTH), 0.02),
        'lru_lam': lam,
        'hy_conv_w': nrm((L, HY_SHORT, 3 * HY_WIDTH), HY_SHORT ** -0.5),
        'hy_conv_b': nrm((L, 3 * HY_WIDTH), 0.02),
        'hy_w1': nrm((L, HY_EMB, HY_HID), HY_EMB ** -0.5),
        'hy_b1': nrm((L, HY_HID), 0.02),
        'hy_w2': nrm((L, HY_INNER, HY_HID, HY_HID), HY_HID ** -0.5),
        'hy_b2': nrm((L, HY_INNER, HY_HID), 0.02),
        'hy_freq': gain((L, HY_HID)),
        'hy_w_out': nrm((L, HY_HID, 2 * HY_ORDER * HY_WIDTH), 0.005),
        'hy_skip': nrm((L, HY_ORDER, HY_WIDTH), 0.3),
        'w_br_a': nrm((L, MLA_HEADS * V_HEAD, D), (MLA_HEADS * V_HEAD) ** -0.5),
        'w_br_b': nrm((L, LRU_WIDTH, D), LRU_WIDTH ** -0.5),
        'w_br_c': nrm((L, HY_WIDTH, D), HY_WIDTH ** -0.5),
        'w_out': nrm((L, D, D), D ** -0.5),
        'ffn_w_gate': nrm((L, D, FFN_HID), D ** -0.5),
        'ffn_w_up': nrm((L, D, FFN_HID), D ** -0.5),
        'ffn_w_down': nrm((L, FFN_HID, D), FFN_HID ** -0.5),
        'final_norm_g': gain((D,)),
    }


def reference(x, c, ctx, c_ctx, ada_w, ada_b, norm1_g, norm2_g, w_in, q_norm_g, w_uq, kv_norm_g, w_ukv,
              lru_conv_w, lru_conv_b, lru_wa, lru_ba, lru_wx, lru_bx, lru_lam,
              hy_conv_w, hy_conv_b, hy_w1, hy_b1, hy_w2, hy_b2, hy_freq, hy_w_out, hy_skip,
              w_br_a, w_br_b, w_br_c, w_out, ffn_w_gate, ffn_w_up, ffn_w_down, final_norm_g):
    xc = ctx
    for l in range(DEPTH):
        p = dict(ada_w=ada_w[l], ada_b=ada_b[l], norm1_g=norm1_g[l], norm2_g=norm2_g[l], w_in=w_in[l],
                 q_norm_g=q_norm_g[l], w_uq=w_uq[l], kv_norm_g=kv_norm_g[l], w_ukv=w_ukv[l],
                 lru_conv_w=lru_conv_w[l], lru_conv_b=lru_conv_b[l], lru_wa=lru_wa[l], lru_ba=lru_ba[l],
                 lru_wx=lru_wx[l], lru_bx=lru_bx[l], lru_lam=lru_lam[l],
                 hy_conv_w=hy_conv_w[l], hy_conv_b=hy_conv_b[l], hy_w1=hy_w1[l], hy_b1=hy_b1[l],
                 hy_w2=hy_w2[l], hy_b2=hy_b2[l], hy_freq=hy_freq[l], hy_w_out=hy_w_out[l], hy_skip=hy_skip[l],
                 w_br_a=w_br_a[l], w_br_b=w_br_b[l], w_br_c=w_br_c[l], w_out=w_out[l],
                 ffn_w_gate=ffn_w_gate[l], ffn_w_up=ffn_w_up[l], ffn_w_down=ffn_w_down[l])
        x, xc = trunk_layer(x, xc, c, c_ctx, p, l < DEPTH - 1)
    return rmsnorm(x, final_norm_g)
```

```python
import math
import os
import numpy as np
import ml_dtypes
from contextlib import ExitStack
import concourse.bass as bass
import concourse.mybir as mybir
from concourse.bass_utils import run_bass_kernel_spmd

F32 = mybir.dt.float32
BF16 = mybir.dt.bfloat16
AF = mybir.ActivationFunctionType
ALU = mybir.AluOpType
AX = mybir.AxisListType

D = 1024
SEQ = 4096
CTX = 256
TT = SEQ + CTX
NL = 2
IN_W = 6304
FFN = 2816
EPS = 1e-6
SM_SCALE = 96 ** -0.5
TILES = [(0, 256)] + [(256 + 512 * i, 512) for i in range(8)]
NFC = 4224
NFW = 4608


class T:
    __slots__ = ("name", "w", "r")

    def __init__(self, name=""):
        self.name = name
        self.w = None
        self.r = {}


class Sched:
    NDMA = 24

    def __init__(self, nc, es):
        self.nc = nc
        self.E = {"pe": nc.tensor, "act": nc.scalar, "dve": nc.vector, "pool": nc.gpsimd, "sp": nc.sync}
        self.sem = {e: es.enter_context(nc.semaphore("c_" + e)) for e in ("pe", "act", "dve", "pool")}
        self.cnt = {e: 0 for e in self.sem}
        self.pend = {e: False for e in self.sem}
        self.dsem = [es.enter_context(nc.semaphore("d%d" % i)) for i in range(self.NDMA)]
        self.dcnt = [0] * self.NDMA
        self.dpool = {"sp": list(range(0, int(os.environ.get("KNSEM", "14"))))}
        self.dnext = {"sp": 0}
        self.pending = []
        self.waited = {e: {} for e in self.E}
        self.nins = 0

    def _wait(self, eng, tok):
        kind, key, val = tok
        if kind == "c":
            if key == eng and eng == "pe":
                return
            sem = self.sem[key]
        else:
            sem = self.dsem[key]
        k = (kind, key)
        if self.waited[eng].get(k, 0) >= val:
            return
        if kind == "c":
            assert val <= self.cnt[key], "wait on un-materialised count"
        self.E[eng].wait_ge(sem, val)
        self.waited[eng][k] = val
        self.nins += 1

    def _deps(self, eng, reads, writes):
        for t in reads:
            if t.w is not None:
                self._wait(eng, t.w)
        for t in writes:
            if t.w is not None:
                self._wait(eng, t.w)
            for (kd, ky), v in t.r.items():
                self._wait(eng, (kd, ky, v))

    def _mark(self, tok, reads, writes):
        for t in reads:
            t.r[(tok[0], tok[1])] = tok[2]
        for t in writes:
            t.w = tok
            t.r = {}

    def op(self, eng, fn, reads=(), writes=(), inc=True):
        self._flush_conflicts(reads, writes)
        self._deps(eng, reads, writes)
        ins = fn()
        self.nins += 1
        if inc:
            self.cnt[eng] += 1
            ins.then_inc(self.sem[eng], 1)
            self._mark(("c", eng, self.cnt[eng]), reads, writes)
        else:
            assert eng == "pe"
            self._mark(("c", eng, self.cnt[eng] + 1), reads, writes)

    def dma(self, out, in_, reads=(), writes=(), eng="sp"):
        if eng != "sp" and not os.environ.get("KNODEFER"):
            self.pending.append([out, in_, list(reads), list(writes), 0])
            return
        self._flush_conflicts(reads, writes)
        self._issue_dma(out, in_, reads, writes)
        for p in self.pending:
            p[4] += 1
        while self.pending and self.pending[0][4] >= 3:
            self._flush(1)

    def _flush(self, n):
        for _ in range(n):
            out, in_, reads, writes, _age = self.pending.pop(0)
            self._issue_dma(out, in_, reads, writes)

    def _flush_conflicts(self, reads, writes):
        if not self.pending:
            return
        hit = -1
        for i, p in enumerate(self.pending):
            pr, pw = p[2], p[3]
            if any(t in pr for t in writes) or any(t in pw for t in reads) or any(t in pw for t in writes):
                hit = i
        if hit >= 0:
            self._flush(hit + 1)

    def _issue_dma(self, out, in_, reads, writes):
        eng = "sp"
        pl = self.dpool[eng]
        k = pl[self.dnext[eng] % len(pl)]
        self.dnext[eng] += 1
        if self.dcnt[k] > 0:
            self._wait(eng, ("d", k, self.dcnt[k]))
        self._deps(eng, reads, writes)
        self.dcnt[k] += 16
        self.E[eng].dma_start(out=out, in_=in_).then_inc(self.dsem[k], 16)
        self.nins += 1
        self._mark(("d", k, self.dcnt[k]), reads, writes)

    def barrier(self):
        self._flush(len(self.pending))
        for eng in self.E:
            for e2 in self.sem:
                if self.cnt[e2] > 0 and e2 != eng:
                    self._wait(eng, ("c", e2, self.cnt[e2]))
            for k in range(self.NDMA):
                if self.dcnt[k] > 0:
                    self._wait(eng, ("d", k, self.dcnt[k]))

    def finish(self):
        self._flush(len(self.pending))
        for k in range(self.NDMA):
            if self.dcnt[k] > 0:
                self._wait("sp", ("d", k, self.dcnt[k]))
        for e2 in self.sem:
            if self.cnt[e2] > 0:
                self._wait("sp", ("c", e2, self.cnt[e2]))


SP_FIELDS = [("ada_b", 48), ("n1g", 8), ("n2g", 8), ("qng", 3), ("kvng", 2), ("lcw", 16), ("lcb", 4),
             ("lba", 8), ("lbx", 8), ("llam", 8), ("hcw", 36), ("hcb", 12), ("hb1", 1), ("hb2", 2),
             ("hfr", 1), ("hskip", 8), ("fng", 8)]
SP_OFF = {}
_o = 0
for _n, _w in SP_FIELDS:
    SP_OFF[_n] = _o
    _o += _w
NS = _o


def pack_small(inp, l):
    sp = np.zeros((128, NS), np.float32)

    def put(name, arr):
        o = SP_OFF[name]
        arr = np.asarray(arr, np.float32)
        sp[:arr.shape[0], o:o + arr.shape[1]] = arr

    cm = lambda v: np.asarray(v, np.float32).reshape(-1, 128).T
    put("ada_b", cm(inp["ada_b"][l]))
    put("n1g", cm(inp["norm1_g"][l]))
    put("n2g", cm(inp["norm2_g"][l]))
    put("qng", cm(inp["q_norm_g"][l]))
    put("kvng", cm(inp["kv_norm_g"][l]))
    lcw = np.asarray(inp["lru_conv_w"][l])
    put("lcw", lcw.reshape(4, 4, 128).transpose(2, 1, 0).reshape(128, 16))
    put("lcb", cm(inp["lru_conv_b"][l]))
    for nm, key in (("lba", "lru_ba"), ("lbx", "lru_bx"), ("llam", "lru_lam")):
        a = np.asarray(inp[key][l])
        put(nm, a.reshape(2, 4, 128).transpose(2, 0, 1).reshape(128, 8))
    hcw = np.asarray(inp["hy_conv_w"][l])
    put("hcw", hcw.reshape(3, 12, 128).transpose(2, 1, 0).reshape(128, 36))
    put("hcb", cm(inp["hy_conv_b"][l]))
    put("hb1", np.asarray(inp["hy_b1"][l]).reshape(64, 1))
    put("hb2", np.asarray(inp["hy_b2"][l]).T)
    put("hfr", np.asarray(inp["hy_freq"][l]).reshape(64, 1))
    hs = np.asarray(inp["hy_skip"][l])
    put("hskip", hs.reshape(2, 4, 128).transpose(2, 0, 1).reshape(128, 8))
    put("fng", cm(inp["final_norm_g"]))
    return sp


_CONST = {}


def host_consts():
    if _CONST:
        return _CONST
    bf = ml_dtypes.bfloat16
    seg = 16
    inv = 1.0 / (10000.0 ** (np.arange(seg // 2, dtype=np.float32) * 2.0 / seg))
    t = np.arange(SEQ)
    row = (t // 64).astype(np.float32)
    col = (t % 64).astype(np.float32)
    cosT = np.zeros((32, SEQ), np.float32)
    sinT = np.zeros((32, SEQ), np.float32)
    for s, pos in ((0, row), (1, col)):
        ang = (pos[:, None] * inv[None, :]).astype(np.float32)
        c = np.cos(ang).T
        sn = np.sin(ang).T
        cosT[s * 16:s * 16 + 8] = c
        cosT[s * 16 + 8:s * 16 + 16] = c
        sinT[s * 16:s * 16 + 8] = -sn
        sinT[s * 16 + 8:s * 16 + 16] = sn
    rope = np.zeros((128, 2, SEQ), np.float32)
    for base in (0, 64):
        rope[base:base + 32, 0] = cosT
        rope[base:base + 32, 1] = sinT
    _CONST["rope"] = rope

    def dft(nrow_pad, ncol_pad, N, nmax):
        a = np.arange(nrow_pad, dtype=np.int64)[:, None]
        b = np.arange(ncol_pad, dtype=np.int64)[None, :]
        ph = ((a * b) % N).astype(np.float64) * (2.0 * np.pi / N)
        msk = (a <= nmax) & (b <= nmax)
        return (np.cos(ph) * msk).astype(bf), (np.sin(ph) * msk).astype(bf)

    def blk(tb):
        R, C = tb.shape[0] // 128, tb.shape[1] // 256
        return np.ascontiguousarray(tb.reshape(R, 128, C, 256).transpose(2, 1, 0, 3))

    def blk512(tb):
        R, C = tb.shape[0] // 128, tb.shape[1] // 512
        return np.ascontiguousarray(tb.reshape(R, 128, C, 512).transpose(2, 1, 0, 3))

    c1, s1 = dft(NFC, NFW, 2 * SEQ, SEQ)
    _CONST["ctab"], _CONST["stab"] = blk(c1), blk(s1)
    c2, s2 = dft(384, 512, 2 * CTX, CTX)
    _CONST["ctab2"], _CONST["stab2"] = blk(c2), blk(s2)

    def wk(N, ncol):
        w = np.zeros((ncol,), np.float32)
        w[:N // 2 + 1] = 2.0 / N
        w[0] = 1.0 / N
        w[N // 2] = 1.0 / N
        return np.ascontiguousarray(np.broadcast_to(w[None, :], (128, ncol)))

    _CONST["wk"] = wk(2 * SEQ, NFW)
    _CONST["wk2"] = wk(2 * CTX, 512)

    def emb(n):
        tt = np.linspace(0.0, 1.0, n, dtype=np.float32)[:, None]
        w = (2.0 * np.float32(math.pi) * np.arange(n, dtype=np.float32) / np.float32(n)).astype(np.float32)
        f = np.linspace(1e-4, 15.0, 16, dtype=np.float32)
        ang = (w[:, None] * f[None, :]).astype(np.float32)
        z = np.concatenate([tt, np.cos(ang), -np.sin(ang)], axis=-1).astype(np.float32)
        max_decay = math.log(1e-2) / 0.3
        min_decay = math.log(1e-2) / 1.5
        deltas = np.abs(np.linspace(min_decay, max_decay, 512, dtype=np.float32))
        dec = np.exp(-tt * deltas[None, :]).astype(np.float32)
        return np.ascontiguousarray(z.T), dec

    _CONST["zemb"], _CONST["decay"] = emb(SEQ)
    _CONST["zemb2"], _CONST["decay2"] = emb(CTX)
    return _CONST


class TG:
    def __init__(self):
        self.d = {}

    def __getitem__(self, k):
        if k not in self.d:
            self.d[k] = T(str(k))
        return self.d[k]


def build(nlayers=NL, dbg=(), stop=None):
    nc = bass.Bass("TRN2", target_bir_lowering=False)

    def din(name, shape, dt=F32):
        return nc.dram_tensor(name, list(shape), dt, kind="ExternalInput").ap()

    def dscr(name, shape, dt=F32):
        return nc.dram_tensor(name, list(shape), dt, kind="Internal").ap()

    dbg_out = {}

    def ddbg(name, shape, dt=F32):
        a = nc.dram_tensor("dbg_" + name, list(shape), dt, kind="ExternalOutput").ap()
        dbg_out[name] = a
        return a

    xT = din("xT", [D, TT])
    cond = din("cond", [128, 16])
    spd = din("sp", [NL, 128, NS])
    ada_w = din("ada_w", [NL, D, 6 * D])
    w_in = din("w_in", [NL, D, IN_W])
    w_kperot = din("w_kperot", [NL, D, 32])
    w_uq = din("w_uq", [NL, 384, 768])
    w_uqrot = din("w_uqrot", [NL, 384, 768])
    w_ukv = din("w_ukv", [NL, 256, 1024])
    lru_bd = din("lru_bd", [NL, 2, 2, 4, 128, 128])
    hy_w1 = din("hy_w1", [NL, 33, 64])
    hy_w2 = din("hy_w2", [NL, 2, 64, 64])
    hy_wo = din("hy_w_out", [NL, 64, 2048])
    w_br = [din("w_br_" + s, [NL, 512, D]) for s in "abc"]
    w_out = din("w_out", [NL, D, D])
    w_fg = din("ffn_w_gate", [NL, D, FFN])
    w_fu = din("ffn_w_up", [NL, D, FFN])
    w_fd = din("ffn_w_down", [NL, FFN, D])
    rope_d = din("rope", [128, 2, SEQ])
    ctab = din("ctab", [NFW // 256, 128, NFC // 128, 256], BF16)
    stab = din("stab", [NFW // 256, 128, NFC // 128, 256], BF16)
    ctab2 = din("ctab2", [2, 128, 3, 256], BF16)
    stab2 = din("stab2", [2, 128, 3, 256], BF16)
    wk_d = din("wk", [128, NFW])
    wk2_d = din("wk2", [128, 512])
    zemb_d = din("zemb", [33, SEQ])
    zemb2_d = din("zemb2", [33, CTX])
    decay_d = din("decay", [SEQ, 512])
    decay2_d = din("decay2", [CTX, 512])
    outT = nc.dram_tensor("outT", [D, SEQ], F32, kind="ExternalOutput").ap()

    Xd = dscr("Xd", [D, TT])
    Zd = dscr("Zd", [27 * 128, TT])
    Gd = dscr("Gd", [3072, TT], BF16)
    Yd = [dscr("Y" + s, [512, TT], BF16) for s in "abc"]
    tX, tZ, tG, tY = TG(), TG(), TG(), [TG(), TG(), TG()]
    Hc = dscr("Hc", [1536, TT])
    Kf = dscr("Kf", [2, 512, NFW])
    Y1 = dscr("Y1", [512, SEQ])
    ACTd = dscr("ACTd", [FFN, TT], BF16)

    with ExitStack() as es:
        S = Sched(nc, es)
        V = lambda fn, r=(), w=(): S.op("dve", fn, r, w)
        A = lambda fn, r=(), w=(): S.op("act", fn, r, w)
        P = lambda fn, r=(), w=(): S.op("pool", fn, r, w)
        M = lambda fn, r=(), w=(), inc=True: S.op("pe", fn, r, w, inc)

        psb = [es.enter_context(nc.psum_tensor("psb%d" % i, [128, 512], F32)) for i in range(7)]
        psn = [0]

        def PSH(stack, dt=BF16):
            psn[0] += 1
            return stack.enter_context(nc.psum_tensor("psx%d" % psn[0], [128, 1024] if dt == BF16 else [128, 512], dt))
        tps = [T("ps%d" % i) for i in range(7)]
        _tp = T("psh"); tpsh = [_tp, _tp]

        sbn = [0]

        def SB(stack, name, shape, dt=F32):
            sbn[0] += 1
            return stack.enter_context(nc.sbuf_tensor("sb%d_%s" % (sbn[0], name), list(shape), dt))

        ones_bf = SB(es, "ones_bf", [128, 128], BF16)
        t_ones = T()
        V(lambda: nc.vector.memset(ones_bf[:], 1.0), (), [t_ones])
        ident_bf = SB(es, "ident_bf", [128, 128], BF16)
        ident_f = SB(es, "ident_f", [128, 128], F32)
        t_ident = T()
        P(lambda: nc.gpsimd.memset(ident_f[:], 0.0), (), [t_ident])
        P(lambda: nc.gpsimd.affine_select(ident_f[:], ident_f[:], [[-1, 128]], ALU.not_equal, 1.0, base=0,
                                          channel_multiplier=1), [t_ident], [t_ident])
        V(lambda: nc.vector.tensor_copy(ident_bf[:], ident_f[:]), [t_ident], [t_ident])
        sp = SB(es, "sp", [128, NL, NS])
        t_sp = T()
        S.dma(sp[:], spd.rearrange("l p n -> p l n"), (), [t_sp])
        mods = SB(es, "mods", [128, NL, 48, 2])
        t_mods = T()
        A1 = SB(es, "A1", [128, NL, 2, 8, 2])
        fA = SB(es, "fA", [128, 8, 2])
        fB = SB(es, "fB", [128, 8, 2])

        def spc(l, name, j=0, n=1):
            o = SP_OFF[name] + j
            return sp[:, l, o:o + n]

        with ExitStack() as ph:
            cs = SB(ph, "cs", [128, 16])
            t_cs = T()
            S.dma(cs[:], cond, (), [t_cs])
            csil = SB(ph, "csil", [128, 16])
            A(lambda: nc.scalar.activation(csil[:], cs[:], AF.Silu), [t_cs], [t_cs])
            wst = [SB(ph, "adaw%d" % i, [128, 8, 512]) for i in range(2)]
            t_wst = [T(), T()]
            for l in range(nlayers):
                pm = psb[0]
                for cb in range(12):
                    b = cb % 2
                    S.dma(wst[b][:], ada_w[l, :, cb * 512:(cb + 1) * 512].rearrange("(k p) n -> p k n", p=128), (), [t_wst[b]])
                    for s4 in range(4):
                        ch = cb * 4 + s4
                        for k in range(8):
                            M(lambda b=b, k=k, s4=s4, ch=ch: nc.tensor.matmul(
                                pm[:, ch * 2:ch * 2 + 2], wst[b][:, k, s4 * 128:(s4 + 1) * 128],
                                csil[:, k * 2:k * 2 + 2], start=(k == 0), stop=(k == 7)),
                              [t_wst[b], t_cs], [tps[0]], inc=(k == 7))
                V(lambda l=l: nc.vector.tensor_tensor(
                    mods[:, l], pm[:, 0:96].rearrange("p (c j) -> p c j", j=2),
                    sp[:, l, SP_OFF["ada_b"]:SP_OFF["ada_b"] + 48].unsqueeze(2).to_broadcast([128, 48, 2]),
                    ALU.add), [tps[0], t_sp], [t_mods])
                for wi, gname in ((0, "n1g"), (1, "n2g")):
                    sc = mods[:, l, (1 + 3 * wi) * 8:(2 + 3 * wi) * 8, :]
                    g = spc(l, gname, 0, 8).unsqueeze(2).to_broadcast([128, 8, 2])
                    V(lambda sc=sc, g=g, l=l, wi=wi: nc.vector.scalar_tensor_tensor(
                        A1[:, l, wi], sc, 1.0, g, ALU.add, ALU.mult), [t_mods, t_sp], [t_mods])
            V(lambda: nc.vector.tensor_copy(fA[:], spc(0, "fng", 0, 8).unsqueeze(2).to_broadcast([128, 8, 2])), [t_sp], [t_mods])
            V(lambda: nc.vector.memset(fB[:], 0.0), (), [t_mods])
            S.barrier()

        stg2 = {}

        def load_w_bf16(ph, name, src_ap, kch, ncol, stage, t_stage):
            wt = SB(ph, name, [128, kch, ncol], BF16)
            tw = T(name)
            src = src_ap.rearrange("(k p) n -> p k n", p=128)
            cw = stage.shape[-1]
            if id(stage) not in stg2:
                stg2[id(stage)] = (SB(ph, name + "_stg", [128, cw]), T())
            stages = [stage, stg2[id(stage)][0]]
            t_stages = [t_stage, stg2[id(stage)][1]]
            i = 0
            for k in range(kch):
                for c0 in range(0, ncol, cw):
                    c1 = min(ncol, c0 + cw)
                    sg_, tsg_ = stages[i % 2], t_stages[i % 2]
                    S.dma(sg_[:, 0:c1 - c0], src[:, k, c0:c1], (), [tsg_])
                    P(lambda k=k, c0=c0, c1=c1, sg_=sg_: nc.gpsimd.tensor_copy(wt[:, k, c0:c1], sg_[:, 0:c1 - c0]), [tsg_], [tw])
                    i += 1
            return wt, tw

        def norm_mod(ph, l, src, src_tok, Aap, Bap, dst, t_dst, tiles, tagp):
            xt = [SB(ph, tagp + "xt%d" % i, [128, 8, 512]) for i in range(2)]
            t_xt = [T(), T()]
            sq = SB(ph, tagp + "sq", [128, 8, 512], BF16)
            t_sq = T()
            rstd = SB(ph, tagp + "rstd", [128, 512])
            t_rstd = T()
            tmp = [SB(ph, tagp + "tmp%d" % i, [128, 512]) for i in range(2)]
            t_tmp = [T(), T()]
            for ti, (t0, n) in enumerate(tiles):
                b = ti % 2
                ty = 1 if t0 < CTX else 0
                S.dma(xt[b][:, :, 0:n], src[:, t0:t0 + n].rearrange("(k p) t -> p k t", p=128), [src_tok[t0]], [t_xt[b]])
                A(lambda b=b, n=n: nc.scalar.activation(sq[:, :, 0:n], xt[b][:, :, 0:n], AF.Square), [t_xt[b]], [t_sq])
                for k in range(8):
                    M(lambda k=k, n=n: nc.tensor.matmul(psb[1][:, 0:n], ones_bf[:], sq[:, k, 0:n], start=(k == 0), stop=(k == 7)),
                      [t_ones, t_sq], [tps[1]], inc=(k == 7))
                V(lambda n=n: nc.vector.tensor_scalar(rstd[:, 0:n], psb[1][:, 0:n], 1.0 / D, EPS, ALU.mult, ALU.add), [tps[1]], [t_rstd])
                A(lambda n=n: nc.scalar.activation(rstd[:, 0:n], rstd[:, 0:n], AF.Sqrt), [t_rstd], [t_rstd])
                V(lambda n=n: nc.vector.reciprocal(rstd[:, 0:n], rstd[:, 0:n]), [t_rstd], [t_rstd])
                for k in range(8):
                    tb = k % 2
                    V(lambda k=k, n=n, b=b, tb=tb, ty=ty: nc.vector.scalar_tensor_tensor(
                        tmp[tb][:, 0:n], xt[b][:, k, 0:n], Aap[:, k, ty:ty + 1], rstd[:, 0:n], ALU.mult, ALU.mult),
                      [t_xt[b], t_rstd, t_mods], [t_tmp[tb]])
                    A(lambda k=k, n=n, tb=tb, ty=ty, t0=t0: nc.scalar.activation(
                        dst[:, k, t0:t0 + n], tmp[tb][:, 0:n], AF.Identity, bias=Bap[:, k, ty:ty + 1], scale=1.0),
                      [t_tmp[tb], t_mods], [t_dst])

        def evac(i, out_ap, in_ap, r, w):
            if i % 2 == 0:
                A(lambda: nc.scalar.copy(out_ap, in_ap), r, w)
            else:
                V(lambda: nc.vector.tensor_copy(out_ap, in_ap), r, w)

        ZCH = []
        for i in range(3):
            ZCH.append(("cq", i, 128 * i, 128))
        for i in range(2):
            ZCH.append(("ckv", 3 + i, 384 + 128 * i, 128))
        ZCH.append(("kpe", 5, 640, 32))
        for i in range(8):
            ZCH.append(("lru", 7 + i, 672 + 128 * i, 128))
        for i in range(12):
            ZCH.append(("hy", 15 + i, 1696 + 128 * i, 128))

        for l in range(nlayers):
            last = (l == NL - 1)
            Xsrc = xT if l == 0 else Xd
            if stop == "A":
                break
            with ExitStack() as ph:
                xn = SB(ph, "xn", [128, 8, TT], BF16)
                t_xn = T()
                with ExitStack() as ph2:
                    norm_mod(ph2, l, Xsrc, tX, A1[:, l, 0], mods[:, l, 0:8, :], xn, t_xn, TILES, "b")
                    S.barrier()
                if stop == "B":
                    break
                if "xn" in dbg and l == 0:
                    dd = ddbg("xn", [D, TT], BF16)
                    S.dma(dd.rearrange("(k p) t -> p k t", p=128), xn[:], [t_xn], [], eng="act")
                wstg = [SB(ph, "wstg%d" % i, [128, 8, 128]) for i in range(2)]
                t_wstg = [T(), T()]
                wbf = [SB(ph, "wbf%d" % i, [128, 8, 128], BF16) for i in range(2)]
                t_wbf = [T(), T()]
                zo = [SB(ph, "zo%d" % i, [128, TT]) for i in range(2)]
                t_zo = [T(), T()]
                gsb = [SB(ph, "gsb%d" % i, [128, TT], BF16) for i in range(2)]
                t_gsb = [T(), T()]
                jobs = [(kind, zr, w_in[l, :, c0:c0 + w], w) for (kind, zr, c0, w) in ZCH]
                jobs.insert(6, ("kperot", 6, w_kperot[l], 32))
                for i in range(24):
                    jobs.append(("gate", i, w_in[l, :, 3232 + 128 * i:3232 + 128 * (i + 1)], 128))
                pi = 0
                for ji, (kind, zr, wsrc, w) in enumerate(jobs):
                    b = ji % 2
                    S.dma(wstg[b][:, :, 0:w], wsrc.rearrange("(k p) n -> p k n", p=128), (), [t_wstg[b]])
                    P(lambda b=b, w=w: nc.gpsimd.tensor_copy(wbf[b][:, :, 0:w], wstg[b][:, :, 0:w]), [t_wstg[b]], [t_wbf[b]])
                    for ti, (t0, n) in enumerate(TILES):
                        pb = 2 + (pi % 4)
                        pi += 1
                        for k in range(8):
                            M(lambda b=b, w=w, k=k, t0=t0, n=n, pb=pb: nc.tensor.matmul(
                                psb[pb][0:w, 0:n], wbf[b][:, k, 0:w], xn[:, k, t0:t0 + n], start=(k == 0), stop=(k == 7)),
                              [t_wbf[b], t_xn], [tps[pb]], inc=(k == 7))
                        if kind == "gate":
                            A(lambda b=b, t0=t0, n=n, pb=pb: nc.scalar.activation(gsb[b][:, t0:t0 + n], psb[pb][:, 0:n], AF.Sigmoid),
                              [tps[pb]], [t_gsb[b]])
                        else:
                            evac(pi, zo[b][0:w, t0:t0 + n], psb[pb][0:w, 0:n], [tps[pb]], [t_zo[b]])
                    if kind == "gate":
                        S.dma(Gd[zr * 128:(zr + 1) * 128, :], gsb[b][:], [t_gsb[b]], [tG[zr]], eng="act")
                    else:
                        S.dma(Zd[zr * 128:zr * 128 + w, :], zo[b][0:w, :], [t_zo[b]], [tZ[zr]], eng="act")
                S.barrier()
            if stop == "C":
                break
            with ExitStack() as ph:
                zx = SB(ph, "zx", [128, TT]); zg = SB(ph, "zg", [128, TT]); u = SB(ph, "u", [128, TT])
                ubf = SB(ph, "ubf", [128, TT], BF16)
                ra = SB(ph, "ra", [128, TT]); gi = SB(ph, "gi", [128, TT])
                hh_ = [SB(ph, "hf", [128, TT]), SB(ph, "hb", [128, TT])]
                ybo = SB(ph, "ybo", [128, TT], BF16)
                bdst = SB(ph, "bdst", [128, 4, 128]); bdbf = SB(ph, "bdbf", [128, 4, 128], BF16)
                c8 = SB(ph, "c8", [128, 2])
                t_zx, t_zg, t_u, t_ubf, t_ra, t_gi, t_ybo, t_bd, t_bdbf, t_c8 = [T() for _ in range(10)]
                t_h = [T(), T()]
                SEGS = ((0, CTX), (CTX, SEQ))
                for j in range(4):
                    S.dma(zx[:], Zd[(7 + j) * 128:(8 + j) * 128, :], (), [t_zx])
                    S.dma(zg[:], Zd[(11 + j) * 128:(12 + j) * 128, :], (), [t_zg])
                    S.dma(bdst[:], lru_bd[l, :, :, j].rearrange("d w p n -> p (d w) n"), (), [t_bd])
                    P(lambda: nc.gpsimd.tensor_copy(bdbf[:], bdst[:]), [t_bd], [t_bdbf])
                    lcw = lambda k, j=j: spc(l, "lcw", j * 4 + k, 1)
                    A(lambda j=j, lcw=lcw: nc.scalar.activation(u[:], zx[:], AF.Identity, bias=spc(l, "lcb", j, 1), scale=lcw(2)),
                      [t_zx, t_sp], [t_u])
                    for k in (0, 1, 3):
                        off = k - 2
                        for (s0, sn) in SEGS:
                            a0 = s0 + max(0, -off)
                            a1 = s0 + sn - max(0, off)
                            V(lambda k=k, a0=a0, a1=a1, off=off, lcw=lcw: nc.vector.scalar_tensor_tensor(
                                u[:, a0:a1], zx[:, a0 + off:a1 + off], lcw(k), u[:, a0:a1], ALU.mult, ALU.add),
                              [t_zx, t_u, t_sp], [t_u])
                    V(lambda: nc.vector.tensor_copy(ubf[:], u[:]), [t_u], [t_ubf])
                    if os.environ.get('KD') == '1':
                        break
                    for d in range(2):
                        lam = spc(l, "llam", d * 4 + j, 1)
                        A(lambda d=d, lam=lam: nc.scalar.activation(c8[:, d:d + 1], lam, AF.Exp, scale=-1.0), [t_sp, t_c8], [t_c8])
                        A(lambda d=d: nc.scalar.add(c8[:, d:d + 1], c8[:, d:d + 1], 1.0), [t_c8], [t_c8])
                        A(lambda d=d: nc.scalar.activation(c8[:, d:d + 1], c8[:, d:d + 1], AF.Ln), [t_c8], [t_c8])
                        A(lambda d=d: nc.scalar.mul(c8[:, d:d + 1], c8[:, d:d + 1], -8.0), [t_c8], [t_c8])
                    if os.environ.get('KD') == '2':
                        break
                    for d in range(2):
                        for ti, (t0, n) in enumerate(TILES):
                            for wi, (dst, t_dst, bname) in enumerate(((ra, t_ra, "lba"), (gi, t_gi, "lbx"))):
                                pb = 2 + ((ti * 2 + wi) % 4)
                                M(lambda d=d, wi=wi, t0=t0, n=n, pb=pb: nc.tensor.matmul(
                                    psb[pb][:, 0:n], bdbf[:, d * 2 + wi, :], ubf[:, t0:t0 + n], start=True, stop=True),
                                  [t_bdbf, t_ubf], [tps[pb]])
                                A(lambda d=d, dst=dst, bname=bname, t0=t0, n=n, pb=pb, j=j: nc.scalar.activation(
                                    dst[:, t0:t0 + n], psb[pb][:, 0:n], AF.Sigmoid, bias=spc(l, bname, d * 4 + j, 1), scale=1.0),
                                  [tps[pb], t_sp], [t_dst])
                        if os.environ.get('KD') == '3':
                            continue
                        A(lambda d=d: nc.scalar.activation(ra[:], ra[:], AF.Exp, scale=c8[:, d:d + 1]), [t_ra, t_c8], [t_ra])
                        A(lambda: nc.scalar.activation(zx[:], ra[:], AF.Square), [t_ra, t_zx, t_u], [t_zx])
                        V(lambda: nc.vector.tensor_scalar(zx[:], zx[:], -1.0, 1.0, ALU.mult, ALU.add), [t_zx], [t_zx])
                        A(lambda: nc.scalar.activation(zx[:], zx[:], AF.Sqrt), [t_zx], [t_zx])
                        V(lambda: nc.vector.tensor_mul(gi[:], gi[:], zx[:]), [t_gi, t_zx], [t_gi])
                        V(lambda: nc.vector.tensor_mul(gi[:], gi[:], u[:]), [t_gi, t_u], [t_gi])
                        if os.environ.get('KD') == '4':
                            continue
                        hd = hh_[d]
                        if os.environ.get('KD') == '5' and d == 1:
                            V(lambda hd=hd: nc.vector.memset(hd[:], 0.0), [], [t_h[d]])
                            continue
                        if os.environ.get('KD') == '6' and d == 0:
                            V(lambda hd=hd: nc.vector.memset(hd[:], 0.0), [], [t_h[d]])
                            continue
                        if d == 0:
                            V(lambda hd=hd: nc.vector.tensor_tensor_scan(hd[:], ra[:], gi[:], 0.0, ALU.mult, ALU.add),
                              [t_ra, t_gi], [t_h[d]])
                        else:
                            rv = lambda tl, t0, n: bass.AP(tl.tensor if hasattr(tl, "tensor") else tl, t0 + n - 1, [[TT, 128], [-1, n]])
                            V(lambda hd=hd: nc.vector.tensor_tensor_scan(rv(hd, 0, CTX), rv(ra, 0, CTX), rv(gi, 0, CTX), 0.0, ALU.mult, ALU.add),
                              [t_ra, t_gi], [t_h[d]])
                            V(lambda hd=hd: nc.vector.tensor_tensor_scan(rv(hd, CTX, SEQ), rv(ra, CTX, SEQ), rv(gi, CTX, SEQ), hd[:, 0:1],
                                                                         ALU.mult, ALU.add), [t_ra, t_gi, t_h[d]], [t_h[d]])
                    V(lambda: nc.vector.tensor_add(hh_[0][:], hh_[0][:], hh_[1][:]), [t_h[0], t_h[1]], [t_h[0]])
                    A(lambda: nc.scalar.activation(zg[:], zg[:], AF.Gelu_apprx_tanh), [t_zg], [t_zg])
                    V(lambda: nc.vector.tensor_mul(ybo[:], hh_[0][:], zg[:]), [t_h[0], t_zg], [t_ybo])
                    S.dma(Yd[1][j * 128:(j + 1) * 128, :], ybo[:], [t_ybo], [tY[1][j]], eng="act")
                S.barrier()
            if stop == "D":
                break
            with ExitStack() as ph:
                cqn = SB(ph, "cqn", [128, 3, TT], BF16); ckvn = SB(ph, "ckvn", [128, 2, TT], BF16)
                kper = SB(ph, "kper", [32, TT], BF16)
                t_cqn, t_ckvn, t_kper = T(), T(), T()
                wq_bf = wqr_bf = wkv_bf = None
                with ExitStack() as ph2:
                    stg = SB(ph2, "fstg", [128, 1024]); t_stg = T()
                    zt = [SB(ph2, "fzt%d" % i, [128, 3, 512]) for i in range(2)]; t_zt = [T(), T()]
                    fsq = SB(ph2, "fsq", [128, 3, 512], BF16); t_fsq = T()
                    frs = SB(ph2, "frs", [128, 512]); t_frs = T()
                    kp = [SB(ph2, "fkp%d" % i, [32, 2, 512]) for i in range(2)]; t_kp = [T(), T()]
                    rp = [SB(ph2, "frp%d" % i, [32, 2, 512]) for i in range(2)]; t_rp = [T(), T()]
                    ktmp = SB(ph2, "fktmp", [32, 2, 512]); t_ktmp = T()
                    for (z0, nk, dstt, t_dstt, gname) in ((0, 3, cqn, t_cqn, "qng"), (384, 2, ckvn, t_ckvn, "kvng")):
                        for ti, (t0, n) in enumerate(TILES):
                            b = ti % 2
                            S.dma(zt[b][:, 0:nk, 0:n], Zd[z0:z0 + nk * 128, t0:t0 + n].rearrange("(k p) t -> p k t", p=128), (), [t_zt[b]])
                            A(lambda b=b, nk=nk, n=n: nc.scalar.activation(fsq[:, 0:nk, 0:n], zt[b][:, 0:nk, 0:n], AF.Square), [t_zt[b]], [t_fsq])
                            for k in range(nk):
                                M(lambda k=k, n=n, nk=nk: nc.tensor.matmul(psb[1][:, 0:n], ones_bf[:], fsq[:, k, 0:n], start=(k == 0), stop=(k == nk - 1)),
                                  [t_ones, t_fsq], [tps[1]], inc=(k == nk - 1))
                            V(lambda n=n, nk=nk: nc.vector.tensor_scalar(frs[:, 0:n], psb[1][:, 0:n], 1.0 / (nk * 128), EPS, ALU.mult, ALU.add), [tps[1]], [t_frs])
                            A(lambda n=n: nc.scalar.activation(frs[:, 0:n], frs[:, 0:n], AF.Sqrt), [t_frs], [t_frs])
                            V(lambda n=n: nc.vector.reciprocal(frs[:, 0:n], frs[:, 0:n]), [t_frs], [t_frs])
                            for k in range(nk):
                                V(lambda k=k, n=n, b=b, t0=t0, dstt=dstt, gname=gname: nc.vector.scalar_tensor_tensor(
                                    dstt[:, k, t0:t0 + n], zt[b][:, k, 0:n], spc(l, gname, k, 1), frs[:, 0:n], ALU.mult, ALU.mult),
                                  [t_zt[b], t_frs, t_sp], [t_dstt])
                    for ti, (t0, n) in enumerate(TILES):
                        b = ti % 2
                        S.dma(kp[b][:, 0, 0:n], Zd[640:672, t0:t0 + n], (), [t_kp[b]])
                        S.dma(kp[b][:, 1, 0:n], Zd[768:800, t0:t0 + n], (), [t_kp[b]])
                        if t0 < CTX:
                            V(lambda b=b, n=n, t0=t0: nc.vector.tensor_copy(kper[:, t0:t0 + n], kp[b][:, 0, 0:n]), [t_kp[b]], [t_kper])
                        else:
                            S.dma(rp[b][:, :, 0:n], rope_d[0:32, :, t0 - CTX:t0 - CTX + n], (), [t_rp[b]])
                            V(lambda b=b, n=n: nc.vector.tensor_mul(ktmp[:, :, 0:n], kp[b][:, :, 0:n], rp[b][:, :, 0:n]), [t_kp[b], t_rp[b]], [t_ktmp])
                            V(lambda b=b, n=n, t0=t0: nc.vector.tensor_add(kper[:, t0:t0 + n], ktmp[:, 0, 0:n], ktmp[:, 1, 0:n]), [t_ktmp], [t_kper])
                    S.barrier()
                with ExitStack() as ph2:
                    stg = SB(ph2, "fstg", [128, 1024]); t_stg = T()
                    wq_bf, t_wq = load_w_bf16(ph2, "wq_bf", w_uq[l], 3, 768, stg, t_stg)
                    wqr_bf, t_wqr = load_w_bf16(ph2, "wqr_bf", w_uqrot[l], 3, 768, stg, t_stg)
                    wkv_bf, t_wkv = load_w_bf16(ph2, "wkv_bf", w_ukv[l], 2, 1024, stg, t_stg)
                    Ka = SB(ph2, "Ka", [128, 4, TT], BF16); Qa = SB(ph2, "Qa", [128, 4, TT], BF16)
                    Va = SB(ph2, "Va", [128, 34, 4, 128], BF16)
                    t_Ka, t_Qa, t_Va = T(), T(), T()
                    sqs = SB(ph2, "sqs", [128, 512], BF16); t_sqs = T()
                    kmx = SB(ph2, "kmx", [128, 4, 16]); t_kmx = T()
                    kmax = SB(ph2, "kmax", [128, 4]); t_kmax = T()
                    qn = SB(ph2, "qn", [128, 512]); t_qn = T()
                    rq = [SB(ph2, "rq%d" % i, [128, 2, 512]) for i in range(2)]; t_rq = [T(), T()]
                    qt1 = SB(ph2, "qt1", [128, 512]); qt2 = SB(ph2, "qt2", [128, 512]); t_qt = T()
                    pt = [SB(ph2, "pt%d" % i, [128, 512], BF16) for i in range(3)]; t_pt = [T(), T(), T()]
                    rec = SB(ph2, "rec", [64, 512]); t_rec = T()
                    yo = [SB(ph2, "yo%d" % i, [64, 512], BF16) for i in range(2)]; t_yo = [T(), T()]
                    for hg in range(2):
                        V(lambda: nc.vector.memset(Va[:, :, :, 64:128], 1.0), [t_Va], [t_Va])
                        V(lambda: nc.vector.memset(Ka[96:97, :, :], 1.0), [t_Ka], [t_Ka])
                        V(lambda: nc.vector.memset(kmx[:], 0.0), [t_kmx], [t_kmx])
                        for hh in range(4):
                            h = hg * 4 + hh
                            V(lambda hh=hh: nc.vector.tensor_copy(Ka[64:96, hh, :], kper[0:32, :]), [t_kper, t_Ka], [t_Ka])
                            for ti, (t0, n) in enumerate(TILES):
                                pb = 2 + (ti % 2)
                                for k in range(2):
                                    M(lambda k=k, h=h, t0=t0, n=n, pb=pb: nc.tensor.matmul(
                                        psb[pb][0:64, 0:n], wkv_bf[:, k, h * 128:h * 128 + 64], ckvn[:, k, t0:t0 + n], start=(k == 0), stop=(k == 1)),
                                      [t_wkv, t_ckvn], [tps[pb]], inc=(k == 1))
                                evac(ti, Ka[0:64, hh, t0:t0 + n], psb[pb][0:64, 0:n], [tps[pb], t_Ka], [t_Ka])
                                A(lambda hh=hh, t0=t0, n=n: nc.scalar.activation(sqs[0:96, 0:n], Ka[0:96, hh, t0:t0 + n], AF.Square), [t_Ka], [t_sqs])
                                M(lambda n=n: nc.tensor.matmul(psb[4][:, 0:n], ones_bf[0:96, :], sqs[0:96, 0:n], start=True, stop=True),
                                  [t_ones, t_sqs], [tps[4]])
                                V(lambda hh=hh, ti=ti, n=n: nc.vector.reduce_max(kmx[:, hh, ti:ti + 1], psb[4][:, 0:n], AX.X), [tps[4], t_kmx], [t_kmx])
                        V(lambda: nc.vector.reduce_max(kmax[:], kmx[:], AX.X), [t_kmx], [t_kmax])
                        for c in range(34):
                            pb = 2 + (c % 2)
                            for k in range(2):
                                M(lambda k=k, c=c, pb=pb: nc.tensor.matmul(
                                    psb[pb][:, 0:256], ckvn[:, k, c * 128:(c + 1) * 128],
                                    wkv_bf[:, k, hg * 512:(hg + 1) * 512].rearrange("p (h x) -> p h x", x=128)[:, :, 64:128],
                                    start=(k == 0), stop=(k == 1)), [t_wkv, t_ckvn], [tps[pb]], inc=(k == 1))
                            evac(c, Va[:, c, :, 0:64], psb[pb][:, 0:256].rearrange("p (h x) -> p h x", x=64), [tps[pb], t_Va], [t_Va])
                        for hh in range(4):
                            h = hg * 4 + hh
                            for ti, (t0, n) in enumerate(TILES):
                                b = ti % 2
                                lat = t0 >= CTX
                                for k in range(3):
                                    M(lambda k=k, h=h, t0=t0, n=n: nc.tensor.matmul(
                                        psb[2][0:96, 0:n], wq_bf[:, k, h * 96:(h + 1) * 96], cqn[:, k, t0:t0 + n], start=(k == 0), stop=(k == 2)),
                                      [t_wq, t_cqn], [tps[2]], inc=(k == 2))
                                if lat:
                                    for k in range(3):
                                        M(lambda k=k, h=h, t0=t0, n=n: nc.tensor.matmul(
                                            psb[3][0:96, 0:n], wqr_bf[:, k, h * 96:(h + 1) * 96], cqn[:, k, t0:t0 + n], start=(k == 0), stop=(k == 2)),
                                          [t_wqr, t_cqn], [tps[3]], inc=(k == 2))
                                    S.dma(rq[b][64:96, :, 0:n], rope_d[64:96, :, t0 - CTX:t0 - CTX + n], (), [t_rq[b]])
                                    A(lambda hh=hh, t0=t0, n=n: nc.scalar.copy(Qa[0:64, hh, t0:t0 + n], psb[2][0:64, 0:n]), [tps[2], t_Qa], [t_Qa])
                                    V(lambda b=b, n=n: nc.vector.tensor_mul(qt1[64:96, 0:n], psb[2][64:96, 0:n], rq[b][64:96, 0, 0:n]), [tps[2], t_rq[b], t_qt], [t_qt])
                                    V(lambda b=b, n=n: nc.vector.tensor_mul(qt2[64:96, 0:n], psb[3][64:96, 0:n], rq[b][64:96, 1, 0:n]), [tps[3], t_rq[b], t_qt], [t_qt])
                                    V(lambda hh=hh, t0=t0, n=n: nc.vector.tensor_add(Qa[64:96, hh, t0:t0 + n], qt1[64:96, 0:n], qt2[64:96, 0:n]), [t_qt, t_Qa], [t_Qa])
                                else:
                                    A(lambda hh=hh, t0=t0, n=n: nc.scalar.copy(Qa[0:96, hh, t0:t0 + n], psb[2][0:96, 0:n]), [tps[2], t_Qa], [t_Qa])
                                A(lambda hh=hh, t0=t0, n=n: nc.scalar.activation(sqs[0:96, 0:n], Qa[0:96, hh, t0:t0 + n], AF.Square), [t_Qa, t_sqs], [t_sqs])
                                M(lambda n=n: nc.tensor.matmul(psb[4][:, 0:n], ones_bf[0:96, :], sqs[0:96, 0:n], start=True, stop=True),
                                  [t_ones, t_sqs], [tps[4]])
                                A(lambda n=n, hh=hh: nc.scalar.activation(qn[:, 0:n], psb[4][:, 0:n], AF.Sqrt, scale=kmax[:, hh:hh + 1]), [tps[4], t_qn, t_kmax], [t_qn])
                                V(lambda hh=hh, t0=t0, n=n: nc.vector.tensor_scalar_mul(Qa[96:97, hh, t0:t0 + n], qn[96:97, 0:n], -1.0),
                                  [t_qn, t_Qa], [t_Qa])
                        if "QK" in dbg and l == 0 and hg == 1:
                            S.dma(ddbg("Qa", [128, 4, TT], BF16), Qa[:], [t_Qa], [])
                            S.dma(ddbg("Ka", [128, 4, TT], BF16), Ka[:], [t_Ka], [])
                            S.dma(ddbg("Va", [128, 34, 4, 128], BF16), Va[:], [t_Va], [])
                            S.dma(ddbg("kmax", [128, 4]), kmax[:], [t_kmax], [])
                        ai = 0
                        for hh in range(4):
                            h = hg * 4 + hh
                            for ti, (t0, n) in enumerate(TILES):
                                chunks = range(2) if t0 < CTX else range(34)
                                po = 5 + (ai % 2)
                                ai += 1
                                chunks = list(chunks)
                                nchk = len(chunks)

                                def qk(c, hh=hh, t0=t0, n=n):
                                    pbs = 2 + (c % 3)
                                    M(lambda: nc.tensor.matmul(psb[pbs][:, 0:n], Ka[0:97, hh, c * 128:(c + 1) * 128], Qa[0:97, hh, t0:t0 + n],
                                                               start=True, stop=True), [t_Ka, t_Qa], [tps[pbs]])

                                qk(chunks[0])
                                if nchk > 1:
                                    qk(chunks[1])
                                for ci, c in enumerate(chunks):
                                    pbs = 2 + (c % 3)
                                    pi_ = c % 3
                                    A(lambda n=n, pbs=pbs, pi_=pi_: nc.scalar.activation(pt[pi_][:, 0:n], psb[pbs][:, 0:n], AF.Exp, scale=SM_SCALE),
                                      [tps[pbs]], [t_pt[pi_]])
                                    if ci + 2 < nchk:
                                        qk(chunks[ci + 2])
                                    M(lambda hh=hh, c=c, n=n, po=po, pi_=pi_, ci=ci, last_=(ci == nchk - 1): nc.tensor.matmul(
                                        psb[po][:, 0:n], Va[:, c, hh, :], pt[pi_][:, 0:n], start=(ci == 0), stop=last_),
                                      [t_Va, t_pt[pi_]], [tps[po]], inc=True)
                                yb_ = ai % 2
                                V(lambda n=n, po=po: nc.vector.reciprocal(rec[:, 0:n], psb[po][64:128, 0:n]), [tps[po], t_rec], [t_rec])
                                V(lambda n=n, po=po, yb_=yb_: nc.vector.tensor_mul(yo[yb_][:, 0:n], psb[po][0:64, 0:n], rec[:, 0:n]), [tps[po], t_rec], [t_yo[yb_]])
                                S.dma(Yd[0][h * 64:(h + 1) * 64, t0:t0 + n], yo[yb_][:, 0:n], [t_yo[yb_]], [tY[0][(h, ti)]], eng="act")
                    S.barrier()
            if stop == "F":
                break
            with ExitStack() as ph:
                with ExitStack() as ph2:
                    hz = [SB(ph2, "hz%d" % i, [128, TT]) for i in range(2)]; t_hz = [T(), T()]
                    hu = [SB(ph2, "hu%d" % i, [128, TT]) for i in range(2)]; t_hu = [T(), T()]
                    for j in range(12):
                        b = j % 2
                        S.dma(hz[b][:], Zd[(15 + j) * 128:(16 + j) * 128, :], (), [t_hz[b]])
                        hw = lambda k, j=j: spc(l, "hcw", j * 3 + k, 1)
                        A(lambda b=b, j=j, hw=hw: nc.scalar.activation(hu[b][:], hz[b][:], AF.Identity, bias=spc(l, "hcb", j, 1), scale=hw(1)),
                          [t_hz[b], t_sp], [t_hu[b]])
                        for k in (0, 2):
                            off = k - 1
                            for (s0, sn) in ((0, CTX), (CTX, SEQ)):
                                a0 = s0 + max(0, -off)
                                a1 = s0 + sn - max(0, off)
                                V(lambda b=b, k=k, a0=a0, a1=a1, off=off, hw=hw: nc.vector.scalar_tensor_tensor(
                                    hu[b][:, a0:a1], hz[b][:, a0 + off:a1 + off], hw(k), hu[b][:, a0:a1], ALU.mult, ALU.add),
                                  [t_hz[b], t_hu[b], t_sp], [t_hu[b]])
                        S.dma(Hc[j * 128:(j + 1) * 128, :], hu[b][:], [t_hu[b]], [], eng="act")
                    S.barrier()
                segs = [(SEQ, CTX, ctab, stab, wk_d, zemb_d, decay_d, NFW, ctab, stab, 512)]
                if not last:
                    segs.append((CTX, 0, ctab2, stab2, wk2_d, zemb2_d, decay2_d, 512, ctab2, stab2, 256))
                for (n, tok0, ctb, stb, wkd, zed, decd, nfw, ctbw, stbw, BW) in segs:
                    NT = n // 128
                    KCH = NT + 1
                    nkb = nfw // 256
                    ntb = n // 256
                    PCS = 11
                    Dt = SB(ph, "Dt", [128, NT, 512], BF16); t_Dt = T()
                    H3 = SB(ph, "H3", [64, n]); t_H3 = T()
                    wo = SB(ph, "hwo", [64, 2048]); t_wo = T()
                    negpi = SB(ph, "negpi", [128, 1]); t_np = T()
                    V(lambda: nc.vector.memset(negpi[:], -math.pi), (), [t_np])
                    S.dma(wo[:], hy_wo[l], (), [t_wo])
                    with ExitStack() as ph2:
                        psh = PSH(ph2)
                        vf = SB(ph2, "vf", [128, n]); t_vf = T()
                        vb = SB(ph2, "vb", [128, n], BF16); t_vb = T()
                        for cc in range(4):
                            S.dma(vf[:], Hc[cc * 128:(cc + 1) * 128, tok0:tok0 + n], (), [t_vf])
                            V(lambda: nc.vector.tensor_copy(vb[:], vf[:]), [t_vf, t_vb], [t_vb])
                            for tc in range(NT):
                                hb_ = tc % 2
                                M(lambda tc=tc, hb_=hb_: nc.tensor.transpose(psh[:, hb_ * 512:hb_ * 512 + 128], vb[:, tc * 128:(tc + 1) * 128], ident_bf[:]),
                                  [t_vb, t_ident, tpsh[hb_]], [tpsh[hb_]])
                                evac(tc, Dt[:, tc, cc * 128:(cc + 1) * 128], psh[:, hb_ * 512:hb_ * 512 + 128], [tpsh[hb_], t_Dt], [t_Dt])
                        ze = SB(ph2, "ze", [33, n]); t_ze = T()
                        S.dma(ze[:], zed, (), [t_ze])
                        w1 = SB(ph2, "hw1", [33, 64]); w2 = SB(ph2, "hw2", [64, 2, 64]); t_w12 = T()
                        S.dma(w1[:], hy_w1[l], (), [t_w12])
                        S.dma(w2[:], hy_w2[l].rearrange("j i o -> i j o"), (), [t_w12])
                        frb = SB(ph2, "frb", [64, 3]); t_frb = T()
                        fr = sp[0:64, l, SP_OFF["hfr"]:SP_OFF["hfr"] + 1]
                        V(lambda: nc.vector.tensor_scalar_mul(frb[:, 0:1], sp[0:64, l, SP_OFF["hb1"]:SP_OFF["hb1"] + 1], fr), [t_sp], [t_frb])
                        V(lambda: nc.vector.tensor_scalar_mul(frb[:, 1:3], sp[0:64, l, SP_OFF["hb2"]:SP_OFF["hb2"] + 2], fr), [t_sp, t_frb], [t_frb])
                        hA = SB(ph2, "hA", [64, n]); hB = SB(ph2, "hB", [64, n]); t_hA, t_hB = T(), T()
                        arg = SB(ph2, "harg", [64, 512]); t_arg = T()
                        argk = SB(ph2, "hargk", [64, 512]); argi = SB(ph2, "hargi", [64, 512], mybir.dt.int32)
                        CW = 512 if n >= 512 else n
                        stages = [(w1[:, :], ze, t_ze, hA, t_hA, 0), (w2[:, 0, :], hA, t_hA, hB, t_hB, 1), (w2[:, 1, :], hB, t_hB, H3, t_H3, 2)]
                        for (wl, src_, t_src, dst_, t_dst_, bi) in stages:
                            for c0 in range(0, n, CW):
                                M(lambda wl=wl, src_=src_, c0=c0: nc.tensor.matmul(psb[4][0:64, 0:CW], wl, src_[:, c0:c0 + CW], start=True, stop=True),
                                  [t_w12, t_src], [tps[4]])
                                V(lambda bi=bi: nc.vector.tensor_scalar(arg[:, 0:CW], psb[4][0:64, 0:CW], fr, frb[:, bi:bi + 1], ALU.mult, ALU.add),
                                  [tps[4], t_frb, t_sp, t_arg], [t_arg])
                                V(lambda: nc.vector.tensor_scalar_mul(argk[:, 0:CW], arg[:, 0:CW], 1.0 / (2.0 * math.pi)), [t_arg], [t_arg])
                                V(lambda: nc.vector.tensor_copy(argi[:, 0:CW], argk[:, 0:CW]), [t_arg], [t_arg])
                                V(lambda: nc.vector.tensor_copy(argk[:, 0:CW], argi[:, 0:CW]), [t_arg], [t_arg])
                                V(lambda: nc.vector.scalar_tensor_tensor(arg[:, 0:CW], argk[:, 0:CW], -2.0 * math.pi, arg[:, 0:CW], ALU.mult, ALU.add), [t_arg], [t_arg])
                                V(lambda: nc.vector.tensor_scalar(arg[:, 0:CW], arg[:, 0:CW], math.pi, -math.pi, ALU.min, ALU.max), [t_arg], [t_arg])
                                A(lambda dst_=dst_, c0=c0: nc.scalar.activation(dst_[:, c0:c0 + CW], arg[:, 0:CW], AF.Sin),
                                  [t_arg, t_np], [t_dst_])
                        S.barrier()
                        if "H3" in dbg and l == 0 and n == SEQ:
                            S.dma(ddbg("H3", [64, SEQ]), H3[:], [t_H3], [])
                            S.dma(ddbg("Dt", [128, 32, 512], BF16), Dt[:], [t_Dt], [])
                            S.dma(ddbg("Hc", [1536, TT]), Hc, [], [])
                    TBS = {}

                    def mk_tabs(stack, ntb):
                        TBS["tabs"] = [SB(stack, "tab%d" % i, [128, 2, PCS, 256], BF16) for i in range(ntb)]
                        TBS["t"] = [[T() for _ in range(4)] for _ in range(ntb)]
                        TBS["n"] = ntb

                    tabn = [0]

                    def load_tab(rows0, nrows_ch, col0):
                        tabs, t_tabs = TBS["tabs"], TBS["t"]
                        i = tabn[0] % TBS["n"]
                        tabn[0] += 1
                        for ci_, tb_ in enumerate((ctb, stb)):
                            S.dma(tabs[i][:, ci_, 0:nrows_ch, :], tb_[col0 // 256, :, rows0:rows0 + nrows_ch, :], t_tabs[i][2:4], [t_tabs[i][ci_]])
                        return tabs[i], t_tabs[i][0:2]

                    def pieces(nch, pcs=PCS):
                        return [(a, min(pcs, nch - a)) for a in range(0, nch, pcs)]

                    PCW = (PCS * 256) // BW

                    def load_tabw(rows0, nrows_ch, cblk):
                        tabs, t_tabs = TBS["tabs"], TBS["t"]
                        i = tabn[0] % TBS["n"]
                        tabn[0] += 1
                        view = tabs[i][:].rearrange("p s k c -> p s (k c)")[:, :, 0:nrows_ch * BW].rearrange("p s (k c) -> p s k c", c=BW)
                        nb_ = BW // 256
                        for ci_, tb_ in enumerate((ctbw, stbw)):
                            for b_ in range(nb_):
                                S.dma(view[:, ci_, :, b_ * 256:(b_ + 1) * 256], tb_[cblk * nb_ + b_, :, rows0:rows0 + nrows_ch, :],
                                      [t for j, t in enumerate(t_tabs[i]) if j != ci_ * 2 + b_], [t_tabs[i][ci_ * 2 + b_]])
                        return view, t_tabs[i][0:2 * nb_] if nb_ == 2 else [t_tabs[i][0], t_tabs[i][2]]

                    for o in range(2):
                        if os.environ.get("KE") == "a":
                            break
                        with ExitStack() as ph2:
                            psx = PSH(ph2, F32); t_psx = T()
                            mk_tabs(ph2, 4)
                            accS = [psb[4], psb[5], psb[6], psx]; t_accS = [tps[4], tps[5], tps[6], t_psx]
                            Hs = SB(ph2, "Hs", [128, NT, 512], BF16); Hd = SB(ph2, "Hd", [128, NT, 512], BF16); t_Hsd = T()
                            ph3 = ExitStack()
                            dec = [SB(ph3, "dec%d" % i, [128, 512]) for i in range(2)]; t_dec = [T(), T()]
                            hsum = SB(ph3, "hsum", [128, 512]); hdif = SB(ph3, "hdif", [128, 512]); t_hs = T()
                            for tc in range(NT):
                                b = tc % 2
                                S.dma(dec[b][:], decd[tc * 128:(tc + 1) * 128, :], (), [t_dec[b]])
                                for di in range(2):
                                    cb = di * 2 + o
                                    M(lambda tc=tc, di=di, cb=cb: nc.tensor.matmul(psb[4 + di][:, :], H3[:, tc * 128:(tc + 1) * 128], wo[:, cb * 512:(cb + 1) * 512],
                                                                                   start=True, stop=True), [t_H3, t_wo], [tps[4 + di]])
                                A(lambda: nc.scalar.copy(hsum[:], psb[4][:, :]), [tps[4], t_hs], [t_hs])
                                V(lambda: nc.vector.tensor_sub(hdif[:], hsum[:], psb[5][:, :]), [tps[5], t_hs], [t_hs])
                                V(lambda: nc.vector.tensor_add(hsum[:], hsum[:], psb[5][:, :]), [tps[5], t_hs], [t_hs])
                                V(lambda tc=tc, b=b: nc.vector.tensor_mul(Hs[:, tc, :], hsum[:], dec[b][:]), [t_hs, t_dec[b]], [t_Hsd])
                                P(lambda tc=tc, b=b: nc.gpsimd.tensor_mul(Hd[:, tc, :], hdif[:], dec[b][:]), [t_hs, t_dec[b]], [t_Hsd])
                            S.barrier()
                            ph3.close()
                            wkb = [SB(ph2, "wkb%d" % i, [128, BW]) for i in range(2)]; t_wkb = [T(), T()]
                            kfo = [SB(ph2, "kfo%d" % i, [128, 2, 4, BW]) for i in range(2)]; t_kfo = [T(), T()]
                            for kb in range(nfw // BW):
                                b = kb % 2
                                S.dma(wkb[b][:], wkd[:, kb * BW:(kb + 1) * BW], (), [t_wkb[b]])
                                for (r0, nr) in pieces(NT, PCW):
                                    tb_, t_tb = load_tabw(r0, nr, kb)
                                    for tcl in range(nr):
                                        tc = r0 + tcl
                                        for cc in range(4):
                                            M(lambda tc=tc, tcl=tcl, cc=cc, tb_=tb_: nc.tensor.matmul(
                                                psb[cc][:, 0:BW], Hs[:, tc, cc * 128:(cc + 1) * 128], tb_[:, 0, tcl, :], start=(tc == 0), stop=(tc == NT - 1)),
                                              [t_Hsd] + t_tb, [tps[cc]], inc=False)
                                            M(lambda tc=tc, tcl=tcl, cc=cc, tb_=tb_: nc.tensor.matmul(
                                                accS[cc][:, 0:BW], Hd[:, tc, cc * 128:(cc + 1) * 128], tb_[:, 1, tcl, :], start=(tc == 0), stop=(tc == NT - 1)),
                                              [t_Hsd] + t_tb, [t_accS[cc]], inc=(tcl == nr - 1 and cc == 3))
                                for cc in range(4):
                                    V(lambda cc=cc, b=b: nc.vector.tensor_mul(kfo[b][:, 0, cc, :], psb[cc][:, 0:BW], wkb[b][:]),
                                      [tps[cc], t_wkb[b], t_kfo[b]], [t_kfo[b]])
                                    V(lambda cc=cc, b=b: nc.vector.tensor_mul(kfo[b][:, 1, cc, :], accS[cc][:, 0:BW], wkb[b][:]),
                                      [t_accS[cc], t_wkb[b], t_kfo[b]], [t_kfo[b]])
                                S.dma(Kf[:, :, kb * BW:(kb + 1) * BW].rearrange("s (c p) k -> p s c k", p=128), kfo[b][:], [t_kfo[b]], [], eng="act")
                            S.barrier()
                            if "H3" in dbg and l == 0 and n == SEQ and o == 0:
                                S.dma(ddbg("Hs", [128, 32, 512], BF16), Hs[:], [t_Hsd], [])
                                S.dma(ddbg("Hd", [128, 32, 512], BF16), Hd[:], [t_Hsd], [])
                                S.dma(ddbg("Kf", [2, 512, NFW]), Kf, [], [])
                                S.barrier()
                        if os.environ.get("KE") == "b":
                            break
                        pho = ExitStack()
                        Yt = [SB(pho, "Ytr", [128, KCH, 512], BF16), SB(pho, "Yts", [128, KCH, 512], BF16)]; t_Yt = T()
                        with ExitStack() as ph2:
                            psh = PSH(ph2)
                            mk_tabs(ph2, 4)
                            kfi = [SB(ph2, "kfi%d" % i, [128, 2, 4, 256]) for i in range(2)]; t_kfi = [T(), T()]
                            us4 = SB(ph2, "us", [128, 4, 2, 256]); t_us4 = [T() for _ in range(4)]
                            tm = [SB(ph2, "tm%d" % i, [128, 256]) for i in range(4)]; t_tm = [T() for _ in range(4)]
                            yrs = [SB(ph2, "yrs%d" % i, [128, 4, 256], BF16) for i in range(2)]; t_yrs = [T(), T()]
                            for kb in range(nkb):
                                b = kb % 2
                                S.dma(kfi[b][:], Kf[:, :, kb * 256:(kb + 1) * 256].rearrange("s (c p) k -> p s c k", p=128), (), [t_kfi[b]])
                                for (r0, nr) in pieces(NT):
                                    tb_, t_tb = load_tab(r0, nr, kb * 256)
                                    for tcl in range(nr):
                                        tc = r0 + tcl
                                        for cc in range(4):
                                            M(lambda tc=tc, tcl=tcl, cc=cc, tb_=tb_: nc.tensor.matmul(
                                                psb[cc][:, :].rearrange("p (s k) -> p s k", s=2), Dt[:, tc, cc * 128:(cc + 1) * 128], tb_[:, :, tcl, :],
                                                start=(tc == 0), stop=(tc == NT - 1)),
                                              [t_Dt] + t_tb, [tps[cc]], inc=(tcl == nr - 1 and cc == 3))
                                for cc in range(4):
                                    A(lambda cc=cc: nc.scalar.copy(us4[:, cc], psb[cc][:, :].rearrange("p (s k) -> p s k", s=2)), [tps[cc], t_us4[cc]], [t_us4[cc]])
                                for cc in range(4):
                                    us = us4[:, cc]
                                    t_us = t_us4[cc]
                                    V(lambda cc=cc, b=b, us=us: nc.vector.tensor_mul(tm[0][:], us[:, 0, :], kfi[b][:, 0, cc, :]), [t_us, t_kfi[b], t_tm[0]], [t_tm[0]])
                                    P(lambda cc=cc, b=b, us=us: nc.gpsimd.tensor_mul(tm[1][:], us[:, 1, :], kfi[b][:, 1, cc, :]), [t_us, t_kfi[b], t_tm[1]], [t_tm[1]])
                                    V(lambda cc=cc, b=b: nc.vector.tensor_sub(yrs[0][:, cc, :], tm[0][:], tm[1][:]), [t_tm[0], t_tm[1], t_yrs[0]], [t_yrs[0]])
                                    P(lambda cc=cc, b=b, us=us: nc.gpsimd.tensor_mul(tm[2][:], us[:, 0, :], kfi[b][:, 1, cc, :]), [t_us, t_kfi[b], t_tm[2]], [t_tm[2]])
                                    V(lambda cc=cc, b=b, us=us: nc.vector.tensor_mul(tm[3][:], us[:, 1, :], kfi[b][:, 0, cc, :]), [t_us, t_kfi[b], t_tm[3]], [t_tm[3]])
                                    P(lambda cc=cc, b=b: nc.gpsimd.tensor_add(yrs[1][:, cc, :], tm[2][:], tm[3][:]), [t_tm[2], t_tm[3], t_yrs[1]], [t_yrs[1]])
                                for ks in range(2):
                                    kc = kb * 2 + ks
                                    if kc >= KCH:
                                        continue
                                    for si in range(2):
                                        for cc in range(4):
                                            M(lambda si=si, cc=cc, ks=ks: nc.tensor.transpose(psh[:, si * 512 + cc * 128:si * 512 + (cc + 1) * 128],
                                                                                           yrs[si][:, cc, ks * 128:(ks + 1) * 128], ident_bf[:]),
                                              [t_yrs[si], t_ident, tpsh[si]], [tpsh[si]], inc=(cc == 3))
                                        evac(si, Yt[si][:, kc, :], psh[:, si * 512:(si + 1) * 512], [tpsh[si], t_Yt], [t_Yt])
                            S.barrier()
                            if "H3" in dbg and l == 0 and n == SEQ and o == 0:
                                S.dma(ddbg("Ytr", [128, 33, 512], BF16), Yt[0][:], [t_Yt], [])
                                S.dma(ddbg("Yts", [128, 33, 512], BF16), Yt[1][:], [t_Yt], [])
                                S.barrier()
                        if os.environ.get("KE") == "c":
                            pho.close()
                            break
                        with ExitStack() as ph2:
                            psh = PSH(ph2)
                            mk_tabs(ph2, 3)
                            ub = [SB(ph2, "ub%d" % i, [128, 4, BW]) for i in range(1)] * 2; t_ub = [T()] * 2
                            xg = [SB(ph2, "xg%d" % i, [128, 4, BW]) for i in range(1)] * 2; t_xg = [T()] * 2
                            tt1 = SB(ph2, "tt1", [128, BW]); t_tt1 = T()
                            ysb = [SB(ph2, "ysb%d" % i, [128, 4, BW]) for i in range(2)]; t_ysb = [[T() for _ in range(4)] for _ in range(2)]
                            rbf = [SB(ph2, "rbf%d" % i, [128, 4, BW], BF16) for i in range(2)]; t_rbf = [T(), T()]
                            NSB = BW // 128
                            for tb in range(n // BW):
                                b = tb % 2
                                c0 = tok0 + tb * BW
                                usrc = Hc[0:512, c0:c0 + BW] if o == 0 else Y1[:, tb * BW:(tb + 1) * BW]
                                S.dma(ub[b][:], usrc.rearrange("(c p) t -> p c t", p=128), (), [t_ub[b]])
                                S.dma(xg[b][:], Hc[(1 + o) * 512:(2 + o) * 512, c0:c0 + BW].rearrange("(c p) t -> p c t", p=128), (), [t_xg[b]])
                                for (r0, nr) in pieces(KCH, PCW):
                                    tb_, t_tb = load_tabw(r0, nr, tb)
                                    for kcl in range(nr):
                                        kc = r0 + kcl
                                        for cc in range(4):
                                            for si in range(2):
                                                M(lambda kc=kc, kcl=kcl, cc=cc, si=si, tb_=tb_: nc.tensor.matmul(
                                                    psb[cc][:, 0:BW], Yt[si][:, kc, cc * 128:(cc + 1) * 128], tb_[:, si, kcl, :],
                                                    start=(kc == 0 and si == 0), stop=(kc == KCH - 1 and si == 1)),
                                                  [t_Yt] + t_tb, [tps[cc]], inc=(kcl == nr - 1 and cc == 3 and si == 1))
                                for cc in range(4):
                                    A(lambda cc=cc, b=b: nc.scalar.copy(ysb[b][:, cc, :], psb[cc][:, 0:BW]), [tps[cc], t_ysb[b][cc]], [t_ysb[b][cc]])
                                for cc in range(4):
                                    V(lambda cc=cc, b=b: nc.vector.scalar_tensor_tensor(tt1[:], ub[b][:, cc, :], spc(l, "hskip", o * 4 + cc, 1), ysb[b][:, cc, :],
                                                                                        ALU.mult, ALU.add), [t_ub[b], t_ysb[b][cc], t_sp, t_tt1], [t_tt1])
                                    if o == 0:
                                        V(lambda cc=cc, b=b: nc.vector.tensor_mul(ysb[b][:, cc, :], tt1[:], xg[b][:, cc, :]), [t_tt1, t_xg[b], t_ysb[b][cc]], [t_ysb[b][cc]])
                                        P(lambda cc=cc, b=b: nc.gpsimd.tensor_copy(rbf[b][:, cc, :], ysb[b][:, cc, :]), [t_ysb[b][cc], t_rbf[b]], [t_rbf[b]])
                                    else:
                                        V(lambda cc=cc, b=b: nc.vector.tensor_mul(rbf[b][:, cc, :], tt1[:], xg[b][:, cc, :]), [t_tt1, t_xg[b], t_rbf[b]], [t_rbf[b]])
                                if o == 0:
                                    S.dma(Y1[:, tb * BW:(tb + 1) * BW].rearrange("(c p) t -> p c t", p=128), ysb[b][:], t_ysb[b], [], eng="act")
                                    for ts_ in range(NSB):
                                        tc = tb * NSB + ts_
                                        hf_ = ts_ % 2
                                        for cc in range(4):
                                            M(lambda cc=cc, ts_=ts_, b=b, hf_=hf_: nc.tensor.transpose(psh[:, hf_ * 512 + cc * 128:hf_ * 512 + (cc + 1) * 128],
                                                                                                     rbf[b][:, cc, ts_ * 128:(ts_ + 1) * 128], ident_bf[:]),
                                              [t_rbf[b], t_ident, tpsh[hf_]], [tpsh[hf_]], inc=(cc == 3))
                                        evac(ts_, Dt[:, tc, :], psh[:, hf_ * 512:(hf_ + 1) * 512], [tpsh[hf_], t_Dt], [t_Dt])
                                else:
                                    S.dma(Yd[2][:, c0:c0 + BW].rearrange("(c p) t -> p c t", p=128), rbf[b][:], [t_rbf[b]], [], eng="act")
                            S.barrier()
                            if "H3" in dbg and l == 0 and n == SEQ and o == 0:
                                S.dma(ddbg("Y1", [512, SEQ]), Y1, [], [])
                                S.barrier()
                        pho.close()
                    S.barrier()
            if stop == "E":
                break
            LT = TILES[1:] if last else TILES
            with ExitStack() as ph:
                stg = SB(ph, "gstg", [128, 1024]); t_stg = T()
                wbr = [load_w_bf16(ph, "wbr%d" % i, w_br[i][l], 4, 1024, stg, t_stg) for i in range(3)]
                wo_bf, t_wo_bf = load_w_bf16(ph, "wo_bf", w_out[l], 8, 1024, stg, t_stg)
                yt = [[SB(ph, "gy%d_%d" % (i, j), [128, 4, 512], BF16) for j in range(2)] for i in range(3)]
                t_yt = [[T(), T()] for _ in range(3)]
                gt = [SB(ph, "gg%d" % j, [128, 24, 512], BF16) for j in range(2)]; t_gt = [T(), T()]
                xt = [SB(ph, "gx%d" % j, [128, 8, 512]) for j in range(2)]; t_xt = [T(), T()]
                xo = [SB(ph, "gxo%d" % j, [128, 8, 512]) for j in range(2)]; t_xo = [T(), T()]
                mg = SB(ph, "gmg", [128, 8, 512], BF16); t_mg = T()
                m1 = SB(ph, "gm1", [128, 512]); m2_ = SB(ph, "gm2", [128, 512]); m3 = SB(ph, "gm3", [128, 512]); t_m = [T(), T(), T()]
                for ti, (t0, n) in enumerate(LT):
                    b = ti % 2
                    ty = 1 if t0 < CTX else 0
                    for i in range(3):
                        S.dma(yt[i][b][:, :, 0:n], Yd[i][:, t0:t0 + n].rearrange("(k p) t -> p k t", p=128), (), [t_yt[i][b]])
                    S.dma(gt[b][:, :, 0:n], Gd[:, t0:t0 + n].rearrange("(k p) t -> p k t", p=128), (), [t_gt[b]])
                    S.dma(xt[b][:, :, 0:n], Xsrc[:, t0:t0 + n].rearrange("(k p) t -> p k t", p=128), (), [t_xt[b]])
                    for m in range(8):
                        for i in range(3):
                            for k in range(4):
                                M(lambda i=i, k=k, m=m, n=n, b=b: nc.tensor.matmul(psb[i][:, 0:n], wbr[i][0][:, k, m * 128:(m + 1) * 128], yt[i][b][:, k, 0:n],
                                                                                 start=(k == 0), stop=(k == 3)), [wbr[i][1], t_yt[i][b]], [tps[i]], inc=(k == 3))
                        V(lambda m=m, n=n, b=b: nc.vector.tensor_mul(m1[:, 0:n], psb[0][:, 0:n], gt[b][:, m, 0:n]), [tps[0], t_gt[b], t_m[0]], [t_m[0]])
                        V(lambda m=m, n=n, b=b: nc.vector.tensor_mul(m2_[:, 0:n], psb[1][:, 0:n], gt[b][:, 8 + m, 0:n]), [tps[1], t_gt[b], t_m[1]], [t_m[1]])
                        V(lambda m=m, n=n, b=b: nc.vector.tensor_mul(m3[:, 0:n], psb[2][:, 0:n], gt[b][:, 16 + m, 0:n]), [tps[2], t_gt[b], t_m[2]], [t_m[2]])
                        P(lambda n=n: nc.gpsimd.tensor_add(m1[:, 0:n], m1[:, 0:n], m2_[:, 0:n]), [t_m[0], t_m[1]], [t_m[0]])
                        P(lambda m=m, n=n: nc.gpsimd.tensor_add(mg[:, m, 0:n], m1[:, 0:n], m3[:, 0:n]), [t_m[0], t_m[2], t_mg], [t_mg])
                    for m in range(8):
                        pb = 3 + (m % 3)
                        for k in range(8):
                            M(lambda k=k, m=m, n=n, pb=pb: nc.tensor.matmul(psb[pb][:, 0:n], wo_bf[:, k, m * 128:(m + 1) * 128], mg[:, k, 0:n],
                                                                            start=(k == 0), stop=(k == 7)), [t_wo_bf, t_mg], [tps[pb]], inc=(k == 7))
                        V(lambda m=m, n=n, b=b, pb=pb, ty=ty: nc.vector.scalar_tensor_tensor(
                            xo[b][:, m, 0:n], psb[pb][:, 0:n], mods[:, l, 16 + m, ty:ty + 1], xt[b][:, m, 0:n], ALU.mult, ALU.add),
                          [tps[pb], t_xt[b], t_mods, t_xo[b]], [t_xo[b]])
                    S.dma(Xd[:, t0:t0 + n].rearrange("(k p) t -> p k t", p=128), xo[b][:, :, 0:n], [t_xo[b]], [], eng="act")
                S.barrier()
            if stop == "G":
                break
            with ExitStack() as ph:
                xn2 = SB(ph, "xn2", [128, 8, TT], BF16); t_xn2 = T()
                with ExitStack() as ph2:
                    norm_mod(ph2, l, Xd, tX, A1[:, l, 1], mods[:, l, 24:32, :], xn2, t_xn2, LT, "h")
                    S.barrier()
                with ExitStack() as ph2:
                    wst2 = [SB(ph2, "hws%d" % i, [128, 2, 8, 128]) for i in range(2)]; t_wst2 = [T(), T()]
                    wb2 = [SB(ph2, "hwb%d" % i, [128, 2, 8, 128], BF16) for i in range(2)]; t_wb2 = [T(), T()]
                    sg = [SB(ph2, "hsg%d" % i, [128, 512]) for i in range(2)]; t_sg = [T(), T()]
                    ao = [SB(ph2, "hao%d" % i, [128, TT], BF16) for i in range(2)]; t_ao = [T(), T()]
                    pi = 0
                    for hc in range(22):
                        b = hc % 2
                        for wi, wsrc in enumerate((w_fg, w_fu)):
                            S.dma(wst2[b][:, wi], wsrc[l, :, hc * 128:(hc + 1) * 128].rearrange("(k p) n -> p k n", p=128), (), [t_wst2[b]])
                        P(lambda b=b: nc.gpsimd.tensor_copy(wb2[b][:], wst2[b][:]), [t_wst2[b]], [t_wb2[b]])
                        for ti, (t0, n) in enumerate(LT):
                            pg = (pi % 3) * 2
                            pi += 1
                            for wi in range(2):
                                for k in range(8):
                                    M(lambda b=b, wi=wi, k=k, t0=t0, n=n, pg=pg: nc.tensor.matmul(
                                        psb[pg + wi][:, 0:n], wb2[b][:, wi, k, :], xn2[:, k, t0:t0 + n], start=(k == 0), stop=(k == 7)),
                                      [t_wb2[b], t_xn2], [tps[pg + wi]], inc=(k == 7))
                            sb_ = pi % 2
                            A(lambda n=n, pg=pg, sb_=sb_: nc.scalar.activation(sg[sb_][:, 0:n], psb[pg][:, 0:n], AF.Silu), [tps[pg], t_sg[sb_]], [t_sg[sb_]])
                            V(lambda n=n, pg=pg, sb_=sb_, b=b, t0=t0: nc.vector.tensor_mul(ao[b][:, t0:t0 + n], sg[sb_][:, 0:n], psb[pg + 1][:, 0:n]),
                              [t_sg[sb_], tps[pg + 1], t_ao[b]], [t_ao[b]])
                        S.dma(ACTd[hc * 128:(hc + 1) * 128, :], ao[b][:], [t_ao[b]], [], eng="act")
                    S.barrier()
            with ExitStack() as ph:
                stg = SB(ph, "dstg", [128, 1024]); t_stg = T()
                wd_bf, t_wd = load_w_bf16(ph, "wd_bf", w_fd[l], 22, 1024, stg, t_stg)
                at = [SB(ph, "dat%d" % j, [128, 22, 512], BF16) for j in range(2)]; t_at = [T(), T()]
                xt = [SB(ph, "dx%d" % j, [128, 8, 512]) for j in range(2)]; t_xt = [T(), T()]
                xo = [SB(ph, "dxo%d" % j, [128, 8, 512]) for j in range(2)]; t_xo = [T(), T()]
                for ti, (t0, n) in enumerate(LT):
                    b = ti % 2
                    ty = 1 if t0 < CTX else 0
                    S.dma(at[b][:, :, 0:n], ACTd[:, t0:t0 + n].rearrange("(k p) t -> p k t", p=128), (), [t_at[b]])
                    S.dma(xt[b][:, :, 0:n], Xd[:, t0:t0 + n].rearrange("(k p) t -> p k t", p=128), (), [t_xt[b]])
                    for m in range(8):
                        pb = m % 4
                        for k in range(22):
                            M(lambda k=k, m=m, n=n, pb=pb, b=b: nc.tensor.matmul(psb[pb][:, 0:n], wd_bf[:, k, m * 128:(m + 1) * 128], at[b][:, k, 0:n],
                                                                                 start=(k == 0), stop=(k == 21)), [t_wd, t_at[b]], [tps[pb]], inc=(k == 21))
                        V(lambda m=m, n=n, b=b, pb=pb, ty=ty: nc.vector.scalar_tensor_tensor(
                            xo[b][:, m, 0:n], psb[pb][:, 0:n], mods[:, l, 40 + m, ty:ty + 1], xt[b][:, m, 0:n], ALU.mult, ALU.add),
                          [tps[pb], t_xt[b], t_mods, t_xo[b]], [t_xo[b]])
                    S.dma(Xd[:, t0:t0 + n].rearrange("(k p) t -> p k t", p=128), xo[b][:, :, 0:n], [t_xo[b]], [], eng="act")
                S.barrier()
            if stop == "H":
                break

        if stop is None:
            with ExitStack() as ph:
                xt = [SB(ph, "fx%d" % i, [128, 8, 512]) for i in range(2)]; t_xt = [T(), T()]
                sq = SB(ph, "fsq2", [128, 8, 512], BF16); t_sq = T()
                rstd = SB(ph, "frstd", [128, 512]); t_rstd = T()
                ot = [SB(ph, "fo%d" % i, [128, 8, 512]) for i in range(2)]; t_ot = [T(), T()]
                for ti, (t0, n) in enumerate(TILES[1:]):
                    b = ti % 2
                    S.dma(xt[b][:], Xd[:, t0:t0 + n].rearrange("(k p) t -> p k t", p=128), (), [t_xt[b]])
                    A(lambda b=b: nc.scalar.activation(sq[:], xt[b][:], AF.Square), [t_xt[b], t_sq], [t_sq])
                    for k in range(8):
                        M(lambda k=k: nc.tensor.matmul(psb[1][:, :], ones_bf[:], sq[:, k, :], start=(k == 0), stop=(k == 7)),
                          [t_ones, t_sq], [tps[1]], inc=(k == 7))
                    V(lambda: nc.vector.tensor_scalar(rstd[:], psb[1][:, :], 1.0 / D, EPS, ALU.mult, ALU.add), [tps[1], t_rstd], [t_rstd])
                    A(lambda: nc.scalar.activation(rstd[:], rstd[:], AF.Sqrt), [t_rstd], [t_rstd])
                    V(lambda: nc.vector.reciprocal(rstd[:], rstd[:]), [t_rstd], [t_rstd])
                    for k in range(8):
                        V(lambda k=k, b=b: nc.vector.scalar_tensor_tensor(ot[b][:, k, :], xt[b][:, k, :], spc(0, "fng", k, 1), rstd[:], ALU.mult, ALU.mult),
                          [t_xt[b], t_rstd, t_sp, t_ot[b]], [t_ot[b]])
                    S.dma(outT[:, t0 - CTX:t0 - CTX + n].rearrange("(k p) t -> p k t", p=128), ot[b][:], [t_ot[b]], [], eng="act")
                S.barrier()

        for name in ("Zd", "Gd", "Ya", "Yb", "Yc", "Xd"):
            if name in dbg:
                src = {"Zd": Zd, "Gd": Gd, "Ya": Yd[0], "Yb": Yd[1], "Yc": Yd[2], "Xd": Xd}[name]
                dd = ddbg(name, src.shape, src.dtype if hasattr(src, "dtype") else F32)
                S.dma(dd, src, [], [], eng="sp")
        S.finish()
    return nc, dbg_out


def shared_maps(inp):
    c = host_consts()
    f = lambda k: np.ascontiguousarray(np.asarray(inp[k], np.float32))
    m = {}
    m["sp"] = np.stack([pack_small(inp, l) for l in range(NL)])
    for k in ("ada_w", "w_in", "w_uq", "w_ukv", "hy_w1", "hy_w2", "hy_w_out", "w_br_a", "w_br_b", "w_br_c", "w_out",
              "ffn_w_gate", "ffn_w_up", "ffn_w_down"):
        m[k] = f(k)
    perm = np.concatenate([np.arange(8, 16), np.arange(0, 8), np.arange(24, 32), np.arange(16, 24)])
    m["w_kperot"] = np.ascontiguousarray(m["w_in"][:, :, 640:672][:, :, perm])
    wq = m["w_uq"].reshape(NL, 384, 8, 96)
    wqr = wq.copy()
    wqr[..., 64:96] = wq[..., 64:96][..., perm]
    m["w_uqrot"] = np.ascontiguousarray(wqr.reshape(NL, 384, 768))
    bd = np.zeros((NL, 2, 2, 4, 128, 128), np.float32)
    for wi, key in enumerate(("lru_wa", "lru_wx")):
        w = np.asarray(inp[key], np.float32)
        for j in range(4):
            bd[:, :, wi, j, 0:64, 0:64] = w[:, :, 2 * j]
            bd[:, :, wi, j, 64:128, 64:128] = w[:, :, 2 * j + 1]
    m["lru_bd"] = bd
    for k in ("rope", "ctab", "stab", "ctab2", "stab2", "wk", "wk2", "zemb", "zemb2", "decay", "decay2"):
        m[k] = c[k]
    return m


def core_map(inp, b, shared):
    m = dict(shared)
    x = np.asarray(inp["x"][b], np.float32)
    ctx = np.asarray(inp["ctx"][b], np.float32)
    m["xT"] = np.ascontiguousarray(np.concatenate([ctx, x], axis=0).T)
    cnd = np.stack([np.asarray(inp["c"][b], np.float32), np.asarray(inp["c_ctx"], np.float32)], axis=1)
    m["cond"] = np.ascontiguousarray(cnd.reshape(8, 128, 2).transpose(1, 0, 2).reshape(128, 16))
    return m


_NC = {}


def kernel(**inputs):
    if "nc" not in _NC:
        _NC["nc"] = build()[0]
    nc = _NC["nc"]
    shared = shared_maps(inputs)
    in_maps = [core_map(inputs, i // 2, shared) for i in range(8)]
    res = run_bass_kernel_spmd(nc, in_maps, core_ids=list(range(8)))
    out = np.stack([np.ascontiguousarray(res.results[2 * b]["outT"].T) for b in range(4)], axis=0)
    return out.astype(np.float32)
```

```python
import math
import os
import numpy as np
import ml_dtypes
from contextlib import ExitStack
import concourse.bass as bass
import concourse.mybir as mybir
from concourse.bass_utils import run_bass_kernel_spmd

F32 = mybir.dt.float32
BF16 = mybir.dt.bfloat16
AF = mybir.ActivationFunctionType
ALU = mybir.AluOpType
AX = mybir.AxisListType

D = 1024
SEQ = 4096
CTX = 256
TT = SEQ + CTX
NL = 2
IN_W = 6304
FFN = 2816
EPS = 1e-6
SM_SCALE = 96 ** -0.5
TILES = [(0, 256)] + [(256 + 512 * i, 512) for i in range(8)]
NFC = 4224
NFW = 4608


class T:
    __slots__ = ("name", "w", "r")

    def __init__(self, name=""):
        self.name = name
        self.w = None
        self.r = {}


class Sched:
    NDMA = 24

    def __init__(self, nc, es):
        self.nc = nc
        self.E = {"pe": nc.tensor, "act": nc.scalar, "dve": nc.vector, "pool": nc.gpsimd, "sp": nc.sync}
        self.sem = {e: es.enter_context(nc.semaphore("c_" + e)) for e in ("pe", "act", "dve", "pool")}
        self.cnt = {e: 0 for e in self.sem}
        self.pend = {e: False for e in self.sem}
        self.dsem = [es.enter_context(nc.semaphore("d%d" % i)) for i in range(self.NDMA)]
        self.dcnt = [0] * self.NDMA
        self.dpool = {"sp": list(range(0, int(os.environ.get("KNSEM", "14"))))}
        self.dnext = {"sp": 0}
        self.pending = []
        self.waited = {e: {} for e in self.E}
        self.nins = 0

    def _wait(self, eng, tok):
        kind, key, val = tok
        if kind == "c":
            if key == eng and eng == "pe":
                return
            sem = self.sem[key]
        else:
            sem = self.dsem[key]
        k = (kind, key)
        if self.waited[eng].get(k, 0) >= val:
            return
        if kind == "c":
            assert val <= self.cnt[key], "wait on un-materialised count"
        self.E[eng].wait_ge(sem, val)
        self.waited[eng][k] = val
        self.nins += 1

    def _deps(self, eng, reads, writes):
        for t in reads:
            if t.w is not None:
                self._wait(eng, t.w)
        for t in writes:
            if t.w is not None:
                self._wait(eng, t.w)
            for (kd, ky), v in t.r.items():
                self._wait(eng, (kd, ky, v))

    def _mark(self, tok, reads, writes):
        for t in reads:
            t.r[(tok[0], tok[1])] = tok[2]
        for t in writes:
            t.w = tok
            t.r = {}

    def op(self, eng, fn, reads=(), writes=(), inc=True):
        self._flush_conflicts(reads, writes)
        self._deps(eng, reads, writes)
        ins = fn()
        self.nins += 1
        if inc:
            self.cnt[eng] += 1
            ins.then_inc(self.sem[eng], 1)
            self._mark(("c", eng, self.cnt[eng]), reads, writes)
        else:
            assert eng == "pe"
            self._mark(("c", eng, self.cnt[eng] + 1), reads, writes)

    def dma(self, out, in_, reads=(), writes=(), eng="sp"):
        if eng != "sp" and not os.environ.get("KNODEFER"):
            self.pending.append([out, in_, list(reads), list(writes), 0])
            return
        self._flush_conflicts(reads, writes)
        self._issue_dma(out, in_, reads, writes)
        for p in self.pending:
            p[4] += 1
        while self.pending and self.pending[0][4] >= 3:
            self._flush(1)

    def _flush(self, n):
        for _ in range(n):
            out, in_, reads, writes, _age = self.pending.pop(0)
            self._issue_dma(out, in_, reads, writes)

    def _flush_conflicts(self, reads, writes):
        if not self.pending:
            return
        hit = -1
        for i, p in enumerate(self.pending):
            pr, pw = p[2], p[3]
            if any(t in pr for t in writes) or any(t in pw for t in reads) or any(t in pw for t in writes):
                hit = i
        if hit >= 0:
            self._flush(hit + 1)

    def _issue_dma(self, out, in_, reads, writes):
        eng = "sp"
        pl = self.dpool[eng]
        k = pl[self.dnext[eng] % len(pl)]
        self.dnext[eng] += 1
        if self.dcnt[k] > 0:
            self._wait(eng, ("d", k, self.dcnt[k]))
        self._deps(eng, reads, writes)
        self.dcnt[k] += 16
        self.E[eng].dma_start(out=out, in_=in_).then_inc(self.dsem[k], 16)
        self.nins += 1
        self._mark(("d", k, self.dcnt[k]), reads, writes)

    def barrier(self):
        self._flush(len(self.pending))
        for eng in self.E:
            for e2 in self.sem:
                if self.cnt[e2] > 0 and e2 != eng:
                    self._wait(eng, ("c", e2, self.cnt[e2]))
            for k in range(self.NDMA):
                if self.dcnt[k] > 0:
                    self._wait(eng, ("d", k, self.dcnt[k]))

    def finish(self):
        self._flush(len(self.pending))
        for k in range(self.NDMA):
            if self.dcnt[k] > 0:
                self._wait("sp", ("d", k, self.dcnt[k]))
        for e2 in self.sem:
            if self.cnt[e2] > 0:
                self._wait("sp", ("c", e2, self.cnt[e2]))


SP_FIELDS = [("ada_b", 48), ("n1g", 8), ("n2g", 8), ("qng", 3), ("kvng", 2), ("lcw", 16), ("lcb", 4),
             ("lba", 8), ("lbx", 8), ("llam", 8), ("hcw", 36), ("hcb", 12), ("hb1", 1), ("hb2", 2),
             ("hfr", 1), ("hskip", 8), ("fng", 8)]
SP_OFF = {}
_o = 0
for _n, _w in SP_FIELDS:
    SP_OFF[_n] = _o
    _o += _w
NS = _o


def pack_small(inp, l):
    sp = np.zeros((128, NS), np.float32)

    def put(name, arr):
        o = SP_OFF[name]
        arr = np.asarray(arr, np.float32)
        sp[:arr.shape[0], o:o + arr.shape[1]] = arr

    cm = lambda v: np.asarray(v, np.float32).reshape(-1, 128).T
    put("ada_b", cm(inp["ada_b"][l]))
    put("n1g", cm(inp["norm1_g"][l]))
    put("n2g", cm(inp["norm2_g"][l]))
    put("qng", cm(inp["q_norm_g"][l]))
    put("kvng", cm(inp["kv_norm_g"][l]))
    lcw = np.asarray(inp["lru_conv_w"][l])
    put("lcw", lcw.reshape(4, 4, 128).transpose(2, 1, 0).reshape(128, 16))
    put("lcb", cm(inp["lru_conv_b"][l]))
    for nm, key in (("lba", "lru_ba"), ("lbx", "lru_bx"), ("llam", "lru_lam")):
        a = np.asarray(inp[key][l])
        put(nm, a.reshape(2, 4, 128).transpose(2, 0, 1).reshape(128, 8))
    hcw = np.asarray(inp["hy_conv_w"][l])
    put("hcw", hcw.reshape(3, 12, 128).transpose(2, 1, 0).reshape(128, 36))
    put("hcb", cm(inp["hy_conv_b"][l]))
    put("hb1", np.asarray(inp["hy_b1"][l]).reshape(64, 1))
    put("hb2", np.asarray(inp["hy_b2"][l]).T)
    put("hfr", np.asarray(inp["hy_freq"][l]).reshape(64, 1))
    hs = np.asarray(inp["hy_skip"][l])
    put("hskip", hs.reshape(2, 4, 128).transpose(2, 0, 1).reshape(128, 8))
    put("fng", cm(inp["final_norm_g"]))
    return sp


_CONST = {}


def host_consts():
    if _CONST:
        return _CONST
    bf = ml_dtypes.bfloat16
    seg = 16
    inv = 1.0 / (10000.0 ** (np.arange(seg // 2, dtype=np.float32) * 2.0 / seg))
    t = np.arange(SEQ)
    row = (t // 64).astype(np.float32)
    col = (t % 64).astype(np.float32)
    cosT = np.zeros((32, SEQ), np.float32)
    sinT = np.zeros((32, SEQ), np.float32)
    for s, pos in ((0, row), (1, col)):
        ang = (pos[:, None] * inv[None, :]).astype(np.float32)
        c = np.cos(ang).T
        sn = np.sin(ang).T
        cosT[s * 16:s * 16 + 8] = c
        cosT[s * 16 + 8:s * 16 + 16] = c
        sinT[s * 16:s * 16 + 8] = -sn
        sinT[s * 16 + 8:s * 16 + 16] = sn
    rope = np.zeros((128, 2, SEQ), np.float32)
    for base in (0, 64):
        rope[base:base + 32, 0] = cosT
        rope[base:base + 32, 1] = sinT
    _CONST["rope"] = rope

    def dft(nrow_pad, ncol_pad, N, nmax):
        a = np.arange(nrow_pad, dtype=np.int64)[:, None]
        b = np.arange(ncol_pad, dtype=np.int64)[None, :]
        ph = ((a * b) % N).astype(np.float64) * (2.0 * np.pi / N)
        msk = (a <= nmax) & (b <= nmax)
        return (np.cos(ph) * msk).astype(bf), (np.sin(ph) * msk).astype(bf)

    def blk(tb):
        R, C = tb.shape[0] // 128, tb.shape[1] // 256
        return np.ascontiguousarray(tb.reshape(R, 128, C, 256).transpose(2, 1, 0, 3))

    c1, s1 = dft(NFC, NFW, 2 * SEQ, SEQ)
    _CONST["ctab"], _CONST["stab"] = blk(c1), blk(s1)
    c2, s2 = dft(384, 512, 2 * CTX, CTX)
    _CONST["ctab2"], _CONST["stab2"] = blk(c2), blk(s2)

    def wk(N, ncol):
        w = np.zeros((ncol,), np.float32)
        w[:N // 2 + 1] = 2.0 / N
        w[0] = 1.0 / N
        w[N // 2] = 1.0 / N
        return np.ascontiguousarray(np.broadcast_to(w[None, :], (128, ncol)))

    _CONST["wk"] = wk(2 * SEQ, NFW)
    _CONST["wk2"] = wk(2 * CTX, 512)

    def emb(n):
        tt = np.linspace(0.0, 1.0, n, dtype=np.float32)[:, None]
        w = (2.0 * np.float32(math.pi) * np.arange(n, dtype=np.float32) / np.float32(n)).astype(np.float32)
        f = np.linspace(1e-4, 15.0, 16, dtype=np.float32)
        ang = (w[:, None] * f[None, :]).astype(np.float32)
        z = np.concatenate([tt, np.cos(ang), -np.sin(ang)], axis=-1).astype(np.float32)
        max_decay = math.log(1e-2) / 0.3
        min_decay = math.log(1e-2) / 1.5
        deltas = np.abs(np.linspace(min_decay, max_decay, 512, dtype=np.float32))
        dec = np.exp(-tt * deltas[None, :]).astype(np.float32)
        return np.ascontiguousarray(z.T), dec

    _CONST["zemb"], _CONST["decay"] = emb(SEQ)
    _CONST["zemb2"], _CONST["decay2"] = emb(CTX)
    return _CONST


class TG:
    def __init__(self):
        self.d = {}

    def __getitem__(self, k):
        if k not in self.d:
            self.d[k] = T(str(k))
        return self.d[k]


def build(nlayers=NL, dbg=(), stop=None):
    nc = bass.Bass("TRN2", target_bir_lowering=False)

    def din(name, shape, dt=F32):
        return nc.dram_tensor(name, list(shape), dt, kind="ExternalInput").ap()

    def dscr(name, shape, dt=F32):
        return nc.dram_tensor(name, list(shape), dt, kind="Internal").ap()

    dbg_out = {}

    def ddbg(name, shape, dt=F32):
        a = nc.dram_tensor("dbg_" + name, list(shape), dt, kind="ExternalOutput").ap()
        dbg_out[name] = a
        return a

    xT = din("xT", [D, TT])
    cond = din("cond", [128, 16])
    spd = din("sp", [NL, 128, NS])
    ada_w = din("ada_w", [NL, D, 6 * D])
    w_in = din("w_in", [NL, D, IN_W])
    w_kperot = din("w_kperot", [NL, D, 32])
    w_uq = din("w_uq", [NL, 384, 768])
    w_uqrot = din("w_uqrot", [NL, 384, 768])
    w_ukv = din("w_ukv", [NL, 256, 1024])
    lru_bd = din("lru_bd", [NL, 2, 2, 4, 128, 128])
    hy_w1 = din("hy_w1", [NL, 33, 64])
    hy_w2 = din("hy_w2", [NL, 2, 64, 64])
    hy_wo = din("hy_w_out", [NL, 64, 2048])
    w_br = [din("w_br_" + s, [NL, 512, D]) for s in "abc"]
    w_out = din("w_out", [NL, D, D])
    w_fg = din("ffn_w_gate", [NL, D, FFN])
    w_fu = din("ffn_w_up", [NL, D, FFN])
    w_fd = din("ffn_w_down", [NL, FFN, D])
    rope_d = din("rope", [128, 2, SEQ])
    ctab = din("ctab", [NFW // 256, 128, NFC // 128, 256], BF16)
    stab = din("stab", [NFW // 256, 128, NFC // 128, 256], BF16)
    ctab2 = din("ctab2", [2, 128, 3, 256], BF16)
    stab2 = din("stab2", [2, 128, 3, 256], BF16)
    wk_d = din("wk", [128, NFW])
    wk2_d = din("wk2", [128, 512])
    zemb_d = din("zemb", [33, SEQ])
    zemb2_d = din("zemb2", [33, CTX])
    decay_d = din("decay", [SEQ, 512])
    decay2_d = din("decay2", [CTX, 512])
    outT = nc.dram_tensor("outT", [D, SEQ], F32, kind="ExternalOutput").ap()

    Xd = dscr("Xd", [D, TT])
    Zd = dscr("Zd", [27 * 128, TT])
    Gd = dscr("Gd", [3072, TT], BF16)
    Yd = [dscr("Y" + s, [512, TT], BF16) for s in "abc"]
    tX, tZ, tG, tY = TG(), TG(), TG(), [TG(), TG(), TG()]
    Hc = dscr("Hc", [1536, TT])
    Kf = dscr("Kf", [2, 512, NFW])
    Y1 = dscr("Y1", [512, SEQ])
    ACTd = dscr("ACTd", [FFN, TT], BF16)

    with ExitStack() as es:
        S = Sched(nc, es)
        V = lambda fn, r=(), w=(): S.op("dve", fn, r, w)
        A = lambda fn, r=(), w=(): S.op("act", fn, r, w)
        P = lambda fn, r=(), w=(): S.op("pool", fn, r, w)
        M = lambda fn, r=(), w=(), inc=True: S.op("pe", fn, r, w, inc)

        psb = [es.enter_context(nc.psum_tensor("psb%d" % i, [128, 512], F32)) for i in range(7)]
        psn = [0]

        def PSH(stack, dt=BF16):
            psn[0] += 1
            return stack.enter_context(nc.psum_tensor("psx%d" % psn[0], [128, 1024] if dt == BF16 else [128, 512], dt))
        tps = [T("ps%d" % i) for i in range(7)]
        _tp = T("psh"); tpsh = [_tp, _tp]

        sbn = [0]

        def SB(stack, name, shape, dt=F32):
            sbn[0] += 1
            return stack.enter_context(nc.sbuf_tensor("sb%d_%s" % (sbn[0], name), list(shape), dt))

        ones_bf = SB(es, "ones_bf", [128, 128], BF16)
        t_ones = T()
        V(lambda: nc.vector.memset(ones_bf[:], 1.0), (), [t_ones])
        ident_bf = SB(es, "ident_bf", [128, 128], BF16)
        ident_f = SB(es, "ident_f", [128, 128], F32)
        t_ident = T()
        P(lambda: nc.gpsimd.memset(ident_f[:], 0.0), (), [t_ident])
        P(lambda: nc.gpsimd.affine_select(ident_f[:], ident_f[:], [[-1, 128]], ALU.not_equal, 1.0, base=0,
                                          channel_multiplier=1), [t_ident], [t_ident])
        V(lambda: nc.vector.tensor_copy(ident_bf[:], ident_f[:]), [t_ident], [t_ident])
        sp = SB(es, "sp", [128, NL, NS])
        t_sp = T()
        S.dma(sp[:], spd.rearrange("l p n -> p l n"), (), [t_sp])
        mods = SB(es, "mods", [128, NL, 48, 2])
        t_mods = T()
        A1 = SB(es, "A1", [128, NL, 2, 8, 2])
        fA = SB(es, "fA", [128, 8, 2])
        fB = SB(es, "fB", [128, 8, 2])

        def spc(l, name, j=0, n=1):
            o = SP_OFF[name] + j
            return sp[:, l, o:o + n]

        with ExitStack() as ph:
            cs = SB(ph, "cs", [128, 16])
            t_cs = T()
            S.dma(cs[:], cond, (), [t_cs])
            csil = SB(ph, "csil", [128, 16])
            A(lambda: nc.scalar.activation(csil[:], cs[:], AF.Silu), [t_cs], [t_cs])
            wst = [SB(ph, "adaw%d" % i, [128, 8, 512]) for i in range(2)]
            t_wst = [T(), T()]
            for l in range(nlayers):
                pm = psb[0]
                for cb in range(12):
                    b = cb % 2
                    S.dma(wst[b][:], ada_w[l, :, cb * 512:(cb + 1) * 512].rearrange("(k p) n -> p k n", p=128), (), [t_wst[b]])
                    for s4 in range(4):
                        ch = cb * 4 + s4
                        for k in range(8):
                            M(lambda b=b, k=k, s4=s4, ch=ch: nc.tensor.matmul(
                                pm[:, ch * 2:ch * 2 + 2], wst[b][:, k, s4 * 128:(s4 + 1) * 128],
                                csil[:, k * 2:k * 2 + 2], start=(k == 0), stop=(k == 7)),
                              [t_wst[b], t_cs], [tps[0]], inc=(k == 7))
                V(lambda l=l: nc.vector.tensor_tensor(
                    mods[:, l], pm[:, 0:96].rearrange("p (c j) -> p c j", j=2),
                    sp[:, l, SP_OFF["ada_b"]:SP_OFF["ada_b"] + 48].unsqueeze(2).to_broadcast([128, 48, 2]),
                    ALU.add), [tps[0], t_sp], [t_mods])
                for wi, gname in ((0, "n1g"), (1, "n2g")):
                    sc = mods[:, l, (1 + 3 * wi) * 8:(2 + 3 * wi) * 8, :]
                    g = spc(l, gname, 0, 8).unsqueeze(2).to_broadcast([128, 8, 2])
                    V(lambda sc=sc, g=g, l=l, wi=wi: nc.vector.scalar_tensor_tensor(
                        A1[:, l, wi], sc, 1.0, g, ALU.add, ALU.mult), [t_mods, t_sp], [t_mods])
            V(lambda: nc.vector.tensor_copy(fA[:], spc(0, "fng", 0, 8).unsqueeze(2).to_broadcast([128, 8, 2])), [t_sp], [t_mods])
            V(lambda: nc.vector.memset(fB[:], 0.0), (), [t_mods])
            S.barrier()

        stg2 = {}

        def load_w_bf16(ph, name, src_ap, kch, ncol, stage, t_stage):
            wt = SB(ph, name, [128, kch, ncol], BF16)
            tw = T(name)
            src = src_ap.rearrange("(k p) n -> p k n", p=128)
            cw = stage.shape[-1]
            if id(stage) not in stg2:
                stg2[id(stage)] = (SB(ph, name + "_stg", [128, cw]), T())
            stages = [stage, stg2[id(stage)][0]]
            t_stages = [t_stage, stg2[id(stage)][1]]
            i = 0
            for k in range(kch):
                for c0 in range(0, ncol, cw):
                    c1 = min(ncol, c0 + cw)
                    sg_, tsg_ = stages[i % 2], t_stages[i % 2]
                    S.dma(sg_[:, 0:c1 - c0], src[:, k, c0:c1], (), [tsg_])
                    P(lambda k=k, c0=c0, c1=c1, sg_=sg_: nc.gpsimd.tensor_copy(wt[:, k, c0:c1], sg_[:, 0:c1 - c0]), [tsg_], [tw])
                    i += 1
            return wt, tw

        def norm_mod(ph, l, src, src_tok, Aap, Bap, dst, t_dst, tiles, tagp):
            xt = [SB(ph, tagp + "xt%d" % i, [128, 8, 512]) for i in range(2)]
            t_xt = [T(), T()]
            sq = SB(ph, tagp + "sq", [128, 8, 512], BF16)
            t_sq = T()
            rstd = SB(ph, tagp + "rstd", [128, 512])
            t_rstd = T()
            tmp = [SB(ph, tagp + "tmp%d" % i, [128, 512]) for i in range(2)]
            t_tmp = [T(), T()]
            for ti, (t0, n) in enumerate(tiles):
                b = ti % 2
                ty = 1 if t0 < CTX else 0
                S.dma(xt[b][:, :, 0:n], src[:, t0:t0 + n].rearrange("(k p) t -> p k t", p=128), [src_tok[t0]], [t_xt[b]])
                A(lambda b=b, n=n: nc.scalar.activation(sq[:, :, 0:n], xt[b][:, :, 0:n], AF.Square), [t_xt[b]], [t_sq])
                for k in range(8):
                    M(lambda k=k, n=n: nc.tensor.matmul(psb[1][:, 0:n], ones_bf[:], sq[:, k, 0:n], start=(k == 0), stop=(k == 7)),
                      [t_ones, t_sq], [tps[1]], inc=(k == 7))
                V(lambda n=n: nc.vector.tensor_scalar(rstd[:, 0:n], psb[1][:, 0:n], 1.0 / D, EPS, ALU.mult, ALU.add), [tps[1]], [t_rstd])
                A(lambda n=n: nc.scalar.activation(rstd[:, 0:n], rstd[:, 0:n], AF.Sqrt), [t_rstd], [t_rstd])
                V(lambda n=n: nc.vector.reciprocal(rstd[:, 0:n], rstd[:, 0:n]), [t_rstd], [t_rstd])
                for k in range(8):
                    tb = k % 2
                    V(lambda k=k, n=n, b=b, tb=tb, ty=ty: nc.vector.scalar_tensor_tensor(
                        tmp[tb][:, 0:n], xt[b][:, k, 0:n], Aap[:, k, ty:ty + 1], rstd[:, 0:n], ALU.mult, ALU.mult),
                      [t_xt[b], t_rstd, t_mods], [t_tmp[tb]])
                    A(lambda k=k, n=n, tb=tb, ty=ty, t0=t0: nc.scalar.activation(
                        dst[:, k, t0:t0 + n], tmp[tb][:, 0:n], AF.Identity, bias=Bap[:, k, ty:ty + 1], scale=1.0),
                      [t_tmp[tb], t_mods], [t_dst])

        def evac(i, out_ap, in_ap, r, w):
            if i % 2 == 0:
                A(lambda: nc.scalar.copy(out_ap, in_ap), r, w)
            else:
                V(lambda: nc.vector.tensor_copy(out_ap, in_ap), r, w)

        ZCH = []
        for i in range(3):
            ZCH.append(("cq", i, 128 * i, 128))
        for i in range(2):
            ZCH.append(("ckv", 3 + i, 384 + 128 * i, 128))
        ZCH.append(("kpe", 5, 640, 32))
        for i in range(8):
            ZCH.append(("lru", 7 + i, 672 + 128 * i, 128))
        for i in range(12):
            ZCH.append(("hy", 15 + i, 1696 + 128 * i, 128))

        for l in range(nlayers):
            last = (l == NL - 1)
            Xsrc = xT if l == 0 else Xd
            if stop == "A":
                break
            with ExitStack() as ph:
                xn = SB(ph, "xn", [128, 8, TT], BF16)
                t_xn = T()
                with ExitStack() as ph2:
                    norm_mod(ph2, l, Xsrc, tX, A1[:, l, 0], mods[:, l, 0:8, :], xn, t_xn, TILES, "b")
                    S.barrier()
                if stop == "B":
                    break
                if "xn" in dbg and l == 0:
                    dd = ddbg("xn", [D, TT], BF16)
                    S.dma(dd.rearrange("(k p) t -> p k t", p=128), xn[:], [t_xn], [], eng="act")
                wstg = [SB(ph, "wstg%d" % i, [128, 8, 128]) for i in range(2)]
                t_wstg = [T(), T()]
                wbf = [SB(ph, "wbf%d" % i, [128, 8, 128], BF16) for i in range(2)]
                t_wbf = [T(), T()]
                zo = [SB(ph, "zo%d" % i, [128, TT]) for i in range(2)]
                t_zo = [T(), T()]
                gsb = [SB(ph, "gsb%d" % i, [128, TT], BF16) for i in range(2)]
                t_gsb = [T(), T()]
                jobs = [(kind, zr, w_in[l, :, c0:c0 + w], w) for (kind, zr, c0, w) in ZCH]
                jobs.insert(6, ("kperot", 6, w_kperot[l], 32))
                for i in range(24):
                    jobs.append(("gate", i, w_in[l, :, 3232 + 128 * i:3232 + 128 * (i + 1)], 128))
                pi = 0
                for ji, (kind, zr, wsrc, w) in enumerate(jobs):
                    b = ji % 2
                    S.dma(wstg[b][:, :, 0:w], wsrc.rearrange("(k p) n -> p k n", p=128), (), [t_wstg[b]])
                    P(lambda b=b, w=w: nc.gpsimd.tensor_copy(wbf[b][:, :, 0:w], wstg[b][:, :, 0:w]), [t_wstg[b]], [t_wbf[b]])
                    for ti, (t0, n) in enumerate(TILES):
                        pb = 2 + (pi % 4)
                        pi += 1
                        for k in range(8):
                            M(lambda b=b, w=w, k=k, t0=t0, n=n, pb=pb: nc.tensor.matmul(
                                psb[pb][0:w, 0:n], wbf[b][:, k, 0:w], xn[:, k, t0:t0 + n], start=(k == 0), stop=(k == 7)),
                              [t_wbf[b], t_xn], [tps[pb]], inc=(k == 7))
                        if kind == "gate":
                            A(lambda b=b, t0=t0, n=n, pb=pb: nc.scalar.activation(gsb[b][:, t0:t0 + n], psb[pb][:, 0:n], AF.Sigmoid),
                              [tps[pb]], [t_gsb[b]])
                        else:
                            evac(pi, zo[b][0:w, t0:t0 + n], psb[pb][0:w, 0:n], [tps[pb]], [t_zo[b]])
                    if kind == "gate":
                        S.dma(Gd[zr * 128:(zr + 1) * 128, :], gsb[b][:], [t_gsb[b]], [tG[zr]], eng="act")
                    else:
                        S.dma(Zd[zr * 128:zr * 128 + w, :], zo[b][0:w, :], [t_zo[b]], [tZ[zr]], eng="act")
                S.barrier()
            if stop == "C":
                break
            with ExitStack() as ph:
                zx = SB(ph, "zx", [128, TT]); zg = SB(ph, "zg", [128, TT]); u = SB(ph, "u", [128, TT])
                ubf = SB(ph, "ubf", [128, TT], BF16)
                ra = SB(ph, "ra", [128, TT]); gi = SB(ph, "gi", [128, TT])
                hh_ = [SB(ph, "hf", [128, TT]), SB(ph, "hb", [128, TT])]
                ybo = SB(ph, "ybo", [128, TT], BF16)
                bdst = SB(ph, "bdst", [128, 4, 128]); bdbf = SB(ph, "bdbf", [128, 4, 128], BF16)
                c8 = SB(ph, "c8", [128, 2])
                t_zx, t_zg, t_u, t_ubf, t_ra, t_gi, t_ybo, t_bd, t_bdbf, t_c8 = [T() for _ in range(10)]
                t_h = [T(), T()]
                SEGS = ((0, CTX), (CTX, SEQ))
                for j in range(4):
                    S.dma(zx[:], Zd[(7 + j) * 128:(8 + j) * 128, :], (), [t_zx])
                    S.dma(zg[:], Zd[(11 + j) * 128:(12 + j) * 128, :], (), [t_zg])
                    S.dma(bdst[:], lru_bd[l, :, :, j].rearrange("d w p n -> p (d w) n"), (), [t_bd])
                    P(lambda: nc.gpsimd.tensor_copy(bdbf[:], bdst[:]), [t_bd], [t_bdbf])
                    lcw = lambda k, j=j: spc(l, "lcw", j * 4 + k, 1)
                    A(lambda j=j, lcw=lcw: nc.scalar.activation(u[:], zx[:], AF.Identity, bias=spc(l, "lcb", j, 1), scale=lcw(2)),
                      [t_zx, t_sp], [t_u])
                    for k in (0, 1, 3):
                        off = k - 2
                        for (s0, sn) in SEGS:
                            a0 = s0 + max(0, -off)
                            a1 = s0 + sn - max(0, off)
                            V(lambda k=k, a0=a0, a1=a1, off=off, lcw=lcw: nc.vector.scalar_tensor_tensor(
                                u[:, a0:a1], zx[:, a0 + off:a1 + off], lcw(k), u[:, a0:a1], ALU.mult, ALU.add),
                              [t_zx, t_u, t_sp], [t_u])
                    V(lambda: nc.vector.tensor_copy(ubf[:], u[:]), [t_u], [t_ubf])
                    if os.environ.get('KD') == '1':
                        break
                    for d in range(2):
                        lam = spc(l, "llam", d * 4 + j, 1)
                        A(lambda d=d, lam=lam: nc.scalar.activation(c8[:, d:d + 1], lam, AF.Exp, scale=-1.0), [t_sp, t_c8], [t_c8])
                        A(lambda d=d: nc.scalar.add(c8[:, d:d + 1], c8[:, d:d + 1], 1.0), [t_c8], [t_c8])
                        A(lambda d=d: nc.scalar.activation(c8[:, d:d + 1], c8[:, d:d + 1], AF.Ln), [t_c8], [t_c8])
                        A(lambda d=d: nc.scalar.mul(c8[:, d:d + 1], c8[:, d:d + 1], -8.0), [t_c8], [t_c8])
                    if os.environ.get('KD') == '2':
                        break
                    for d in range(2):
                        for ti, (t0, n) in enumerate(TILES):
                            for wi, (dst, t_dst, bname) in enumerate(((ra, t_ra, "lba"), (gi, t_gi, "lbx"))):
                                pb = 2 + ((ti * 2 + wi) % 4)
                                M(lambda d=d, wi=wi, t0=t0, n=n, pb=pb: nc.tensor.matmul(
                                    psb[pb][:, 0:n], bdbf[:, d * 2 + wi, :], ubf[:, t0:t0 + n], start=True, stop=True),
                                  [t_bdbf, t_ubf], [tps[pb]])
                                A(lambda d=d, dst=dst, bname=bname, t0=t0, n=n, pb=pb, j=j: nc.scalar.activation(
                                    dst[:, t0:t0 + n], psb[pb][:, 0:n], AF.Sigmoid, bias=spc(l, bname, d * 4 + j, 1), scale=1.0),
                                  [tps[pb], t_sp], [t_dst])
                        if os.environ.get('KD') == '3':
                            continue
                        A(lambda d=d: nc.scalar.activation(ra[:], ra[:], AF.Exp, scale=c8[:, d:d + 1]), [t_ra, t_c8], [t_ra])
                        A(lambda: nc.scalar.activation(zx[:], ra[:], AF.Square), [t_ra, t_zx, t_u], [t_zx])
                        V(lambda: nc.vector.tensor_scalar(zx[:], zx[:], -1.0, 1.0, ALU.mult, ALU.add), [t_zx], [t_zx])
                        A(lambda: nc.scalar.activation(zx[:], zx[:], AF.Sqrt), [t_zx], [t_zx])
                        V(lambda: nc.vector.tensor_mul(gi[:], gi[:], zx[:]), [t_gi, t_zx], [t_gi])
                        V(lambda: nc.vector.tensor_mul(gi[:], gi[:], u[:]), [t_gi, t_u], [t_gi])
                        if os.environ.get('KD') == '4':
                            continue
                        hd = hh_[d]
                        if os.environ.get('KD') == '5' and d == 1:
                            V(lambda hd=hd: nc.vector.memset(hd[:], 0.0), [], [t_h[d]])
                            continue
                        if os.environ.get('KD') == '6' and d == 0:
                            V(lambda hd=hd: nc.vector.memset(hd[:], 0.0), [], [t_h[d]])
                            continue
                        if d == 0:
                            V(lambda hd=hd: nc.vector.tensor_tensor_scan(hd[:], ra[:], gi[:], 0.0, ALU.mult, ALU.add),
                              [t_ra, t_gi], [t_h[d]])
                        else:
                            rv = lambda tl, t0, n: bass.AP(tl.tensor if hasattr(tl, "tensor") else tl, t0 + n - 1, [[TT, 128], [-1, n]])
                            V(lambda hd=hd: nc.vector.tensor_tensor_scan(rv(hd, 0, CTX), rv(ra, 0, CTX), rv(gi, 0, CTX), 0.0, ALU.mult, ALU.add),
                              [t_ra, t_gi], [t_h[d]])
                            V(lambda hd=hd: nc.vector.tensor_tensor_scan(rv(hd, CTX, SEQ), rv(ra, CTX, SEQ), rv(gi, CTX, SEQ), hd[:, 0:1],
                                                                         ALU.mult, ALU.add), [t_ra, t_gi, t_h[d]], [t_h[d]])
                    V(lambda: nc.vector.tensor_add(hh_[0][:], hh_[0][:], hh_[1][:]), [t_h[0], t_h[1]], [t_h[0]])
                    A(lambda: nc.scalar.activation(zg[:], zg[:], AF.Gelu_apprx_tanh), [t_zg], [t_zg])
                    V(lambda: nc.vector.tensor_mul(ybo[:], hh_[0][:], zg[:]), [t_h[0], t_zg], [t_ybo])
                    S.dma(Yd[1][j * 128:(j + 1) * 128, :], ybo[:], [t_ybo], [tY[1][j]], eng="act")
                S.barrier()
            if stop == "D":
                break
            with ExitStack() as ph:
                cqn = SB(ph, "cqn", [128, 3, TT], BF16); ckvn = SB(ph, "ckvn", [128, 2, TT], BF16)
                kper = SB(ph, "kper", [32, TT], BF16)
                t_cqn, t_ckvn, t_kper = T(), T(), T()
                wq_bf = wqr_bf = wkv_bf = None
                with ExitStack() as ph2:
                    stg = SB(ph2, "fstg", [128, 1024]); t_stg = T()
                    zt = [SB(ph2, "fzt%d" % i, [128, 3, 512]) for i in range(2)]; t_zt = [T(), T()]
                    fsq = SB(ph2, "fsq", [128, 3, 512], BF16); t_fsq = T()
                    frs = SB(ph2, "frs", [128, 512]); t_frs = T()
                    kp = [SB(ph2, "fkp%d" % i, [32, 2, 512]) for i in range(2)]; t_kp = [T(), T()]
                    rp = [SB(ph2, "frp%d" % i, [32, 2, 512]) for i in range(2)]; t_rp = [T(), T()]
                    ktmp = SB(ph2, "fktmp", [32, 2, 512]); t_ktmp = T()
                    for (z0, nk, dstt, t_dstt, gname) in ((0, 3, cqn, t_cqn, "qng"), (384, 2, ckvn, t_ckvn, "kvng")):
                        for ti, (t0, n) in enumerate(TILES):
                            b = ti % 2
                            S.dma(zt[b][:, 0:nk, 0:n], Zd[z0:z0 + nk * 128, t0:t0 + n].rearrange("(k p) t -> p k t", p=128), (), [t_zt[b]])
                            A(lambda b=b, nk=nk, n=n: nc.scalar.activation(fsq[:, 0:nk, 0:n], zt[b][:, 0:nk, 0:n], AF.Square), [t_zt[b]], [t_fsq])
                            for k in range(nk):
                                M(lambda k=k, n=n, nk=nk: nc.tensor.matmul(psb[1][:, 0:n], ones_bf[:], fsq[:, k, 0:n], start=(k == 0), stop=(k == nk - 1)),
                                  [t_ones, t_fsq], [tps[1]], inc=(k == nk - 1))
                            V(lambda n=n, nk=nk: nc.vector.tensor_scalar(frs[:, 0:n], psb[1][:, 0:n], 1.0 / (nk * 128), EPS, ALU.mult, ALU.add), [tps[1]], [t_frs])
                            A(lambda n=n: nc.scalar.activation(frs[:, 0:n], frs[:, 0:n], AF.Sqrt), [t_frs], [t_frs])
                            V(lambda n=n: nc.vector.reciprocal(frs[:, 0:n], frs[:, 0:n]), [t_frs], [t_frs])
                            for k in range(nk):
                                V(lambda k=k, n=n, b=b, t0=t0, dstt=dstt, gname=gname: nc.vector.scalar_tensor_tensor(
                                    dstt[:, k, t0:t0 + n], zt[b][:, k, 0:n], spc(l, gname, k, 1), frs[:, 0:n], ALU.mult, ALU.mult),
                                  [t_zt[b], t_frs, t_sp], [t_dstt])
                    for ti, (t0, n) in enumerate(TILES):
                        b = ti % 2
                        S.dma(kp[b][:, 0, 0:n], Zd[640:672, t0:t0 + n], (), [t_kp[b]])
                        S.dma(kp[b][:, 1, 0:n], Zd[768:800, t0:t0 + n], (), [t_kp[b]])
                        if t0 < CTX:
                            V(lambda b=b, n=n, t0=t0: nc.vector.tensor_copy(kper[:, t0:t0 + n], kp[b][:, 0, 0:n]), [t_kp[b]], [t_kper])
                        else:
                            S.dma(rp[b][:, :, 0:n], rope_d[0:32, :, t0 - CTX:t0 - CTX + n], (), [t_rp[b]])
                            V(lambda b=b, n=n: nc.vector.tensor_mul(ktmp[:, :, 0:n], kp[b][:, :, 0:n], rp[b][:, :, 0:n]), [t_kp[b], t_rp[b]], [t_ktmp])
                            V(lambda b=b, n=n, t0=t0: nc.vector.tensor_add(kper[:, t0:t0 + n], ktmp[:, 0, 0:n], ktmp[:, 1, 0:n]), [t_ktmp], [t_kper])
                    S.barrier()
                with ExitStack() as ph2:
                    stg = SB(ph2, "fstg", [128, 1024]); t_stg = T()
                    wq_bf, t_wq = load_w_bf16(ph2, "wq_bf", w_uq[l], 3, 768, stg, t_stg)
                    wqr_bf, t_wqr = load_w_bf16(ph2, "wqr_bf", w_uqrot[l], 3, 768, stg, t_stg)
                    wkv_bf, t_wkv = load_w_bf16(ph2, "wkv_bf", w_ukv[l], 2, 1024, stg, t_stg)
                    Ka = SB(ph2, "Ka", [128, 4, TT], BF16); Qa = SB(ph2, "Qa", [128, 4, TT], BF16)
                    Va = SB(ph2, "Va", [128, 34, 4, 128], BF16)
                    t_Ka, t_Qa, t_Va = T(), T(), T()
                    fK, fQ = TG(), TG()
                    jn = SB(ph2, "jn", [128, 2]); t_jn = T()
                    sqs = SB(ph2, "sqs", [128, 512], BF16); t_sqs = T()
                    kmx = SB(ph2, "kmx", [128, 4, 16]); t_kmx = T()
                    kmax = SB(ph2, "kmax", [128, 4]); t_kmax = T()
                    qn = SB(ph2, "qn", [128, 512]); t_qn = T()
                    rq = [SB(ph2, "rq%d" % i, [128, 2, 512]) for i in range(2)]; t_rq = [T(), T()]
                    qt1 = SB(ph2, "qt1", [128, 512]); qt2 = SB(ph2, "qt2", [128, 512]); t_qt = T()
                    pt = [SB(ph2, "pt%d" % i, [128, 512], BF16) for i in range(3)]; t_pt = [T(), T(), T()]
                    rec = SB(ph2, "rec", [64, 512]); t_rec = T()
                    yo = [SB(ph2, "yo%d" % i, [64, 512], BF16) for i in range(2)]; t_yo = [T(), T()]
                    for hg in range(2):
                        V(lambda: nc.vector.memset(Va[:, :, :, 64:128], 1.0), [t_Va], [t_Va])
                        V(lambda: nc.vector.memset(Ka[96:97, :, :], 1.0), [t_Ka], [t_Ka])
                        V(lambda: nc.vector.memset(jn[:, 1:2], 0.0), [t_jn], [t_Qa, t_jn])
                        V(lambda: nc.vector.memset(kmx[:], 0.0), [t_kmx], [t_kmx])
                        for hh in range(4):
                            h = hg * 4 + hh
                            V(lambda hh=hh: nc.vector.tensor_copy(Ka[64:96, hh, :], kper[0:32, :]), [t_kper, t_Ka], [t_Ka])
                            for ti, (t0, n) in enumerate(TILES):
                                pb = 2 + (ti % 2)
                                for k in range(2):
                                    M(lambda k=k, h=h, t0=t0, n=n, pb=pb: nc.tensor.matmul(
                                        psb[pb][0:64, 0:n], wkv_bf[:, k, h * 128:h * 128 + 64], ckvn[:, k, t0:t0 + n], start=(k == 0), stop=(k == 1)),
                                      [t_wkv, t_ckvn], [tps[pb]], inc=(k == 1))
                                evac(ti, Ka[0:64, hh, t0:t0 + n], psb[pb][0:64, 0:n], [tps[pb], t_Ka], [fK[(hh, ti)]])
                                A(lambda hh=hh, t0=t0, n=n: nc.scalar.activation(sqs[0:96, 0:n], Ka[0:96, hh, t0:t0 + n], AF.Square), [t_Ka, fK[(hh, ti)]], [t_sqs])
                                M(lambda n=n: nc.tensor.matmul(psb[4][:, 0:n], ones_bf[0:96, :], sqs[0:96, 0:n], start=True, stop=True),
                                  [t_ones, t_sqs], [tps[4]])
                                V(lambda hh=hh, ti=ti, n=n: nc.vector.reduce_max(kmx[:, hh, ti:ti + 1], psb[4][:, 0:n], AX.X), [tps[4], t_kmx], [t_kmx])
                        V(lambda: nc.vector.reduce_max(kmax[:], kmx[:], AX.X), [t_kmx], [t_kmax])
                        V(lambda: nc.vector.memset(jn[:, 0:1], 0.0), [fK[(hh_, ti_)] for hh_ in range(4) for ti_ in range(len(TILES))] + [t_jn], [t_Ka, t_jn])
                        for c in range(34):
                            pb = 2 + (c % 2)
                            for k in range(2):
                                M(lambda k=k, c=c, pb=pb: nc.tensor.matmul(
                                    psb[pb][:, 0:256], ckvn[:, k, c * 128:(c + 1) * 128],
                                    wkv_bf[:, k, hg * 512:(hg + 1) * 512].rearrange("p (h x) -> p h x", x=128)[:, :, 64:128],
                                    start=(k == 0), stop=(k == 1)), [t_wkv, t_ckvn], [tps[pb]], inc=(k == 1))
                            evac(c, Va[:, c, :, 0:64], psb[pb][:, 0:256].rearrange("p (h x) -> p h x", x=64), [tps[pb], t_Va], [t_Va])
                        for hh in range(4):
                            h = hg * 4 + hh
                            for ti, (t0, n) in enumerate(TILES):
                                b = ti % 2
                                lat = t0 >= CTX
                                for k in range(3):
                                    M(lambda k=k, h=h, t0=t0, n=n: nc.tensor.matmul(
                                        psb[2][0:96, 0:n], wq_bf[:, k, h * 96:(h + 1) * 96], cqn[:, k, t0:t0 + n], start=(k == 0), stop=(k == 2)),
                                      [t_wq, t_cqn], [tps[2]], inc=(k == 2))
                                if lat:
                                    for k in range(3):
                                        M(lambda k=k, h=h, t0=t0, n=n: nc.tensor.matmul(
                                            psb[3][0:96, 0:n], wqr_bf[:, k, h * 96:(h + 1) * 96], cqn[:, k, t0:t0 + n], start=(k == 0), stop=(k == 2)),
                                          [t_wqr, t_cqn], [tps[3]], inc=(k == 2))
                                    S.dma(rq[b][64:96, :, 0:n], rope_d[64:96, :, t0 - CTX:t0 - CTX + n], (), [t_rq[b]])
                                    A(lambda hh=hh, t0=t0, n=n: nc.scalar.copy(Qa[0:64, hh, t0:t0 + n], psb[2][0:64, 0:n]), [tps[2], t_Qa], [fQ[(hh, ti)]])
                                    V(lambda b=b, n=n: nc.vector.tensor_mul(qt1[64:96, 0:n], psb[2][64:96, 0:n], rq[b][64:96, 0, 0:n]), [tps[2], t_rq[b], t_qt], [t_qt])
                                    V(lambda b=b, n=n: nc.vector.tensor_mul(qt2[64:96, 0:n], psb[3][64:96, 0:n], rq[b][64:96, 1, 0:n]), [tps[3], t_rq[b], t_qt], [t_qt])
                                    V(lambda hh=hh, t0=t0, n=n: nc.vector.tensor_add(Qa[64:96, hh, t0:t0 + n], qt1[64:96, 0:n], qt2[64:96, 0:n]), [t_qt, t_Qa], [fQ[(hh, ti)]])
                                else:
                                    A(lambda hh=hh, t0=t0, n=n: nc.scalar.copy(Qa[0:96, hh, t0:t0 + n], psb[2][0:96, 0:n]), [tps[2], t_Qa], [fQ[(hh, ti)]])
                                A(lambda hh=hh, t0=t0, n=n: nc.scalar.activation(sqs[0:96, 0:n], Qa[0:96, hh, t0:t0 + n], AF.Square), [fQ[(hh, ti)], t_sqs], [t_sqs])
                                M(lambda n=n: nc.tensor.matmul(psb[4][:, 0:n], ones_bf[0:96, :], sqs[0:96, 0:n], start=True, stop=True),
                                  [t_ones, t_sqs], [tps[4]])
                                A(lambda n=n, hh=hh: nc.scalar.activation(qn[:, 0:n], psb[4][:, 0:n], AF.Sqrt, scale=kmax[:, hh:hh + 1]), [tps[4], t_qn, t_kmax], [t_qn])
                                V(lambda hh=hh, t0=t0, n=n: nc.vector.tensor_scalar_mul(Qa[96:97, hh, t0:t0 + n], qn[96:97, 0:n], -1.0),
                                  [t_qn, t_Qa], [fQ[(hh, ti)]])
                        if "QK" in dbg and l == 0 and hg == 1:
                            S.dma(ddbg("Qa", [128, 4, TT], BF16), Qa[:], [t_Qa], [])
                            S.dma(ddbg("Ka", [128, 4, TT], BF16), Ka[:], [t_Ka], [])
                            S.dma(ddbg("Va", [128, 34, 4, 128], BF16), Va[:], [t_Va], [])
                            S.dma(ddbg("kmax", [128, 4]), kmax[:], [t_kmax], [])
                        V(lambda: nc.vector.memset(jn[:, 1:2], 0.0), [fQ[(hh_, ti_)] for hh_ in range(4) for ti_ in range(len(TILES))] + [t_jn], [t_Qa, t_jn])
                        ai = 0
                        for hh in range(4):
                            h = hg * 4 + hh
                            for ti, (t0, n) in enumerate(TILES):
                                chunks = range(2) if t0 < CTX else range(34)
                                po = 5 + (ai % 2)
                                ai += 1
                                chunks = list(chunks)
                                nchk = len(chunks)

                                def qk(c, hh=hh, t0=t0, n=n):
                                    pbs = 2 + (c % 3)
                                    M(lambda: nc.tensor.matmul(psb[pbs][:, 0:n], Ka[0:97, hh, c * 128:(c + 1) * 128], Qa[0:97, hh, t0:t0 + n],
                                                               start=True, stop=True), [t_Ka, t_Qa], [tps[pbs]])

                                qk(chunks[0])
                                if nchk > 1:
                                    qk(chunks[1])
                                for ci, c in enumerate(chunks):
                                    pbs = 2 + (c % 3)
                                    pi_ = c % 3
                                    A(lambda n=n, pbs=pbs, pi_=pi_: nc.scalar.activation(pt[pi_][:, 0:n], psb[pbs][:, 0:n], AF.Exp, scale=SM_SCALE),
                                      [tps[pbs]], [t_pt[pi_]])
                                    if ci + 2 < nchk:
                                        qk(chunks[ci + 2])
                                    M(lambda hh=hh, c=c, n=n, po=po, pi_=pi_, ci=ci, last_=(ci == nchk - 1): nc.tensor.matmul(
                                        psb[po][:, 0:n], Va[:, c, hh, :], pt[pi_][:, 0:n], start=(ci == 0), stop=last_),
                                      [t_Va, t_pt[pi_]], [tps[po]], inc=True)
                                yb_ = ai % 2
                                V(lambda n=n, po=po: nc.vector.reciprocal(rec[:, 0:n], psb[po][64:128, 0:n]), [tps[po], t_rec], [t_rec])
                                V(lambda n=n, po=po, yb_=yb_: nc.vector.tensor_mul(yo[yb_][:, 0:n], psb[po][0:64, 0:n], rec[:, 0:n]), [tps[po], t_rec], [t_yo[yb_]])
                                S.dma(Yd[0][h * 64:(h + 1) * 64, t0:t0 + n], yo[yb_][:, 0:n], [t_yo[yb_]], [tY[0][(h, ti)]], eng="act")
                    S.barrier()
            if stop == "F":
                break
            with ExitStack() as ph:
                with ExitStack() as ph2:
                    hz = [SB(ph2, "hz%d" % i, [128, TT]) for i in range(2)]; t_hz = [T(), T()]
                    hu = [SB(ph2, "hu%d" % i, [128, TT]) for i in range(2)]; t_hu = [T(), T()]
                    for j in range(12):
                        b = j % 2
                        S.dma(hz[b][:], Zd[(15 + j) * 128:(16 + j) * 128, :], (), [t_hz[b]])
                        hw = lambda k, j=j: spc(l, "hcw", j * 3 + k, 1)
                        A(lambda b=b, j=j, hw=hw: nc.scalar.activation(hu[b][:], hz[b][:], AF.Identity, bias=spc(l, "hcb", j, 1), scale=hw(1)),
                          [t_hz[b], t_sp], [t_hu[b]])
                        for k in (0, 2):
                            off = k - 1
                            for (s0, sn) in ((0, CTX), (CTX, SEQ)):
                                a0 = s0 + max(0, -off)
                                a1 = s0 + sn - max(0, off)
                                V(lambda b=b, k=k, a0=a0, a1=a1, off=off, hw=hw: nc.vector.scalar_tensor_tensor(
                                    hu[b][:, a0:a1], hz[b][:, a0 + off:a1 + off], hw(k), hu[b][:, a0:a1], ALU.mult, ALU.add),
                                  [t_hz[b], t_hu[b], t_sp], [t_hu[b]])
                        S.dma(Hc[j * 128:(j + 1) * 128, :], hu[b][:], [t_hu[b]], [], eng="act")
                    S.barrier()
                segs = [(SEQ, CTX, ctab, stab, wk_d, zemb_d, decay_d, NFW)]
                if not last:
                    segs.append((CTX, 0, ctab2, stab2, wk2_d, zemb2_d, decay2_d, 512))
                for (n, tok0, ctb, stb, wkd, zed, decd, nfw) in segs:
                    NT = n // 128
                    KCH = NT + 1
                    nkb = nfw // 256
                    ntb = n // 256
                    PCS = 11
                    Dt = SB(ph, "Dt", [128, NT, 512], BF16); t_Dt = T()
                    H3 = SB(ph, "H3", [64, n]); t_H3 = T()
                    wo = SB(ph, "hwo", [64, 2048]); t_wo = T()
                    negpi = SB(ph, "negpi", [128, 1]); t_np = T()
                    V(lambda: nc.vector.memset(negpi[:], -math.pi), (), [t_np])
                    S.dma(wo[:], hy_wo[l], (), [t_wo])
                    with ExitStack() as ph2:
                        psh = PSH(ph2)
                        vf = SB(ph2, "vf", [128, n]); t_vf = T()
                        vb = SB(ph2, "vb", [128, n], BF16); t_vb = T()
                        for cc in range(4):
                            S.dma(vf[:], Hc[cc * 128:(cc + 1) * 128, tok0:tok0 + n], (), [t_vf])
                            V(lambda: nc.vector.tensor_copy(vb[:], vf[:]), [t_vf, t_vb], [t_vb])
                            for tc in range(NT):
                                hb_ = tc % 2
                                M(lambda tc=tc, hb_=hb_: nc.tensor.transpose(psh[:, hb_ * 512:hb_ * 512 + 128], vb[:, tc * 128:(tc + 1) * 128], ident_bf[:]),
                                  [t_vb, t_ident, tpsh[hb_]], [tpsh[hb_]])
                                evac(tc, Dt[:, tc, cc * 128:(cc + 1) * 128], psh[:, hb_ * 512:hb_ * 512 + 128], [tpsh[hb_], t_Dt], [t_Dt])
                        ze = SB(ph2, "ze", [33, n]); t_ze = T()
                        S.dma(ze[:], zed, (), [t_ze])
                        w1 = SB(ph2, "hw1", [33, 64]); w2 = SB(ph2, "hw2", [64, 2, 64]); t_w12 = T()
                        S.dma(w1[:], hy_w1[l], (), [t_w12])
                        S.dma(w2[:], hy_w2[l].rearrange("j i o -> i j o"), (), [t_w12])
                        frb = SB(ph2, "frb", [64, 3]); t_frb = T()
                        fr = sp[0:64, l, SP_OFF["hfr"]:SP_OFF["hfr"] + 1]
                        V(lambda: nc.vector.tensor_scalar_mul(frb[:, 0:1], sp[0:64, l, SP_OFF["hb1"]:SP_OFF["hb1"] + 1], fr), [t_sp], [t_frb])
                        V(lambda: nc.vector.tensor_scalar_mul(frb[:, 1:3], sp[0:64, l, SP_OFF["hb2"]:SP_OFF["hb2"] + 2], fr), [t_sp, t_frb], [t_frb])
                        hA = SB(ph2, "hA", [64, n]); hB = SB(ph2, "hB", [64, n]); t_hA, t_hB = T(), T()
                        arg = SB(ph2, "harg", [64, 512]); t_arg = T()
                        argk = SB(ph2, "hargk", [64, 512]); argi = SB(ph2, "hargi", [64, 512], mybir.dt.int32)
                        CW = 512 if n >= 512 else n
                        stages = [(w1[:, :], ze, t_ze, hA, t_hA, 0), (w2[:, 0, :], hA, t_hA, hB, t_hB, 1), (w2[:, 1, :], hB, t_hB, H3, t_H3, 2)]
                        for (wl, src_, t_src, dst_, t_dst_, bi) in stages:
                            for c0 in range(0, n, CW):
                                M(lambda wl=wl, src_=src_, c0=c0: nc.tensor.matmul(psb[4][0:64, 0:CW], wl, src_[:, c0:c0 + CW], start=True, stop=True),
                                  [t_w12, t_src], [tps[4]])
                                V(lambda bi=bi: nc.vector.tensor_scalar(arg[:, 0:CW], psb[4][0:64, 0:CW], fr, frb[:, bi:bi + 1], ALU.mult, ALU.add),
                                  [tps[4], t_frb, t_sp, t_arg], [t_arg])
                                V(lambda: nc.vector.tensor_scalar_mul(argk[:, 0:CW], arg[:, 0:CW], 1.0 / (2.0 * math.pi)), [t_arg], [t_arg])
                                V(lambda: nc.vector.tensor_copy(argi[:, 0:CW], argk[:, 0:CW]), [t_arg], [t_arg])
                                V(lambda: nc.vector.tensor_copy(argk[:, 0:CW], argi[:, 0:CW]), [t_arg], [t_arg])
                                V(lambda: nc.vector.scalar_tensor_tensor(arg[:, 0:CW], argk[:, 0:CW], -2.0 * math.pi, arg[:, 0:CW], ALU.mult, ALU.add), [t_arg], [t_arg])
                                V(lambda: nc.vector.tensor_scalar(arg[:, 0:CW], arg[:, 0:CW], math.pi, -math.pi, ALU.min, ALU.max), [t_arg], [t_arg])
                                A(lambda dst_=dst_, c0=c0: nc.scalar.activation(dst_[:, c0:c0 + CW], arg[:, 0:CW], AF.Sin),
                                  [t_arg, t_np], [t_dst_])
                        S.barrier()
                        if "H3" in dbg and l == 0 and n == SEQ:
                            S.dma(ddbg("H3", [64, SEQ]), H3[:], [t_H3], [])
                            S.dma(ddbg("Dt", [128, 32, 512], BF16), Dt[:], [t_Dt], [])
                            S.dma(ddbg("Hc", [1536, TT]), Hc, [], [])
                    NTB = 4
                    tabs = [SB(ph, "tab%d" % i, [128, 2, PCS, 256], BF16) for i in range(NTB)]; t_tabs = [T() for _ in range(NTB)]
                    tabn = [0]

                    def load_tab(rows0, nrows_ch, col0):
                        i = tabn[0] % NTB
                        tabn[0] += 1
                        for ci_, tb_ in enumerate((ctb, stb)):
                            S.dma(tabs[i][:, ci_, 0:nrows_ch, :], tb_[col0 // 256, :, rows0:rows0 + nrows_ch, :], (), [t_tabs[i]])
                        return tabs[i], t_tabs[i]

                    def pieces(nch):
                        return [(a, min(PCS, nch - a)) for a in range(0, nch, PCS)]

                    for o in range(2):
                        if os.environ.get("KE") == "a":
                            break
                        with ExitStack() as ph2:
                            psx = PSH(ph2, F32); t_psx = T()
                            accS = [psb[4], psb[5], psb[6], psx]; t_accS = [tps[4], tps[5], tps[6], t_psx]
                            Hs = SB(ph2, "Hs", [128, NT, 512], BF16); Hd = SB(ph2, "Hd", [128, NT, 512], BF16); t_Hsd = T()
                            dec = [SB(ph2, "dec%d" % i, [128, 512]) for i in range(2)]; t_dec = [T(), T()]
                            hsum = SB(ph2, "hsum", [128, 512]); hdif = SB(ph2, "hdif", [128, 512]); t_hs = T()
                            for tc in range(NT):
                                b = tc % 2
                                S.dma(dec[b][:], decd[tc * 128:(tc + 1) * 128, :], (), [t_dec[b]])
                                for di in range(2):
                                    cb = di * 2 + o
                                    M(lambda tc=tc, di=di, cb=cb: nc.tensor.matmul(psb[4 + di][:, :], H3[:, tc * 128:(tc + 1) * 128], wo[:, cb * 512:(cb + 1) * 512],
                                                                                   start=True, stop=True), [t_H3, t_wo], [tps[4 + di]])
                                A(lambda: nc.scalar.copy(hsum[:], psb[4][:, :]), [tps[4], t_hs], [t_hs])
                                V(lambda: nc.vector.tensor_sub(hdif[:], hsum[:], psb[5][:, :]), [tps[5], t_hs], [t_hs])
                                V(lambda: nc.vector.tensor_add(hsum[:], hsum[:], psb[5][:, :]), [tps[5], t_hs], [t_hs])
                                V(lambda tc=tc, b=b: nc.vector.tensor_mul(Hs[:, tc, :], hsum[:], dec[b][:]), [t_hs, t_dec[b]], [t_Hsd])
                                P(lambda tc=tc, b=b: nc.gpsimd.tensor_mul(Hd[:, tc, :], hdif[:], dec[b][:]), [t_hs, t_dec[b]], [t_Hsd])
                            wkb = [SB(ph2, "wkb%d" % i, [128, 256]) for i in range(2)]; t_wkb = [T(), T()]
                            kfo = [SB(ph2, "kfo%d" % i, [128, 2, 4, 256]) for i in range(2)]; t_kfo = [T(), T()]
                            for kb in range(nkb):
                                b = kb % 2
                                S.dma(wkb[b][:], wkd[:, kb * 256:(kb + 1) * 256], (), [t_wkb[b]])
                                for (r0, nr) in pieces(NT):
                                    tb_, t_tb = load_tab(r0, nr, kb * 256)
                                    for tcl in range(nr):
                                        tc = r0 + tcl
                                        for cc in range(4):
                                            M(lambda tc=tc, tcl=tcl, cc=cc, tb_=tb_: nc.tensor.matmul(
                                                psb[cc][:, 0:256], Hs[:, tc, cc * 128:(cc + 1) * 128], tb_[:, 0, tcl, :], start=(tc == 0), stop=(tc == NT - 1)),
                                              [t_Hsd, t_tb], [tps[cc]], inc=False)
                                            M(lambda tc=tc, tcl=tcl, cc=cc, tb_=tb_: nc.tensor.matmul(
                                                accS[cc][:, 0:256], Hd[:, tc, cc * 128:(cc + 1) * 128], tb_[:, 1, tcl, :], start=(tc == 0), stop=(tc == NT - 1)),
                                              [t_Hsd, t_tb], [t_accS[cc]], inc=(tcl == nr - 1 and cc == 3))
                                for cc in range(4):
                                    V(lambda cc=cc, b=b: nc.vector.tensor_mul(kfo[b][:, 0, cc, :], psb[cc][:, 0:256], wkb[b][:]),
                                      [tps[cc], t_wkb[b], t_kfo[b]], [t_kfo[b]])
                                    V(lambda cc=cc, b=b: nc.vector.tensor_mul(kfo[b][:, 1, cc, :], accS[cc][:, 0:256], wkb[b][:]),
                                      [t_accS[cc], t_wkb[b], t_kfo[b]], [t_kfo[b]])
                                S.dma(Kf[:, :, kb * 256:(kb + 1) * 256].rearrange("s (c p) k -> p s c k", p=128), kfo[b][:], [t_kfo[b]], [], eng="act")
                            S.barrier()
                            if "H3" in dbg and l == 0 and n == SEQ and o == 0:
                                S.dma(ddbg("Hs", [128, 32, 512], BF16), Hs[:], [t_Hsd], [])
                                S.dma(ddbg("Hd", [128, 32, 512], BF16), Hd[:], [t_Hsd], [])
                                S.dma(ddbg("Kf", [2, 512, NFW]), Kf, [], [])
                                S.barrier()
                        if os.environ.get("KE") == "b":
                            break
                        pho = ExitStack()
                        Yt = [SB(pho, "Ytr", [128, KCH, 512], BF16), SB(pho, "Yts", [128, KCH, 512], BF16)]; t_Yt = T()
                        with ExitStack() as ph2:
                            psh = PSH(ph2)
                            kfi = [SB(ph2, "kfi%d" % i, [128, 2, 4, 256]) for i in range(2)]; t_kfi = [T(), T()]
                            us4 = SB(ph2, "us", [128, 4, 2, 256]); t_us4 = [T() for _ in range(4)]
                            tm = [SB(ph2, "tm%d" % i, [128, 256]) for i in range(4)]; t_tm = [T() for _ in range(4)]
                            yrs = [SB(ph2, "yrs%d" % i, [128, 4, 256], BF16) for i in range(2)]; t_yrs = [T(), T()]
                            for kb in range(nkb):
                                b = kb % 2
                                S.dma(kfi[b][:], Kf[:, :, kb * 256:(kb + 1) * 256].rearrange("s (c p) k -> p s c k", p=128), (), [t_kfi[b]])
                                for (r0, nr) in pieces(NT):
                                    tb_, t_tb = load_tab(r0, nr, kb * 256)
                                    for tcl in range(nr):
                                        tc = r0 + tcl
                                        for cc in range(4):
                                            M(lambda tc=tc, tcl=tcl, cc=cc, tb_=tb_: nc.tensor.matmul(
                                                psb[cc][:, :].rearrange("p (s k) -> p s k", s=2), Dt[:, tc, cc * 128:(cc + 1) * 128], tb_[:, :, tcl, :],
                                                start=(tc == 0), stop=(tc == NT - 1)),
                                              [t_Dt, t_tb], [tps[cc]], inc=(tcl == nr - 1 and cc == 3))
                                for cc in range(4):
                                    A(lambda cc=cc: nc.scalar.copy(us4[:, cc], psb[cc][:, :].rearrange("p (s k) -> p s k", s=2)), [tps[cc], t_us4[cc]], [t_us4[cc]])
                                for cc in range(4):
                                    us = us4[:, cc]
                                    t_us = t_us4[cc]
                                    V(lambda cc=cc, b=b, us=us: nc.vector.tensor_mul(tm[0][:], us[:, 0, :], kfi[b][:, 0, cc, :]), [t_us, t_kfi[b], t_tm[0]], [t_tm[0]])
                                    P(lambda cc=cc, b=b, us=us: nc.gpsimd.tensor_mul(tm[1][:], us[:, 1, :], kfi[b][:, 1, cc, :]), [t_us, t_kfi[b], t_tm[1]], [t_tm[1]])
                                    V(lambda cc=cc, b=b: nc.vector.tensor_sub(yrs[0][:, cc, :], tm[0][:], tm[1][:]), [t_tm[0], t_tm[1], t_yrs[0]], [t_yrs[0]])
                                    P(lambda cc=cc, b=b, us=us: nc.gpsimd.tensor_mul(tm[2][:], us[:, 0, :], kfi[b][:, 1, cc, :]), [t_us, t_kfi[b], t_tm[2]], [t_tm[2]])
                                    V(lambda cc=cc, b=b, us=us: nc.vector.tensor_mul(tm[3][:], us[:, 1, :], kfi[b][:, 0, cc, :]), [t_us, t_kfi[b], t_tm[3]], [t_tm[3]])
                                    P(lambda cc=cc, b=b: nc.gpsimd.tensor_add(yrs[1][:, cc, :], tm[2][:], tm[3][:]), [t_tm[2], t_tm[3], t_yrs[1]], [t_yrs[1]])
                                for ks in range(2):
                                    kc = kb * 2 + ks
                                    if kc >= KCH:
                                        continue
                                    for si in range(2):
                                        for cc in range(4):
                                            M(lambda si=si, cc=cc, ks=ks: nc.tensor.transpose(psh[:, si * 512 + cc * 128:si * 512 + (cc + 1) * 128],
                                                                                           yrs[si][:, cc, ks * 128:(ks + 1) * 128], ident_bf[:]),
                                              [t_yrs[si], t_ident, tpsh[si]], [tpsh[si]], inc=(cc == 3))
                                        evac(si, Yt[si][:, kc, :], psh[:, si * 512:(si + 1) * 512], [tpsh[si], t_Yt], [t_Yt])
                            S.barrier()
                            if "H3" in dbg and l == 0 and n == SEQ and o == 0:
                                S.dma(ddbg("Ytr", [128, 33, 512], BF16), Yt[0][:], [t_Yt], [])
                                S.dma(ddbg("Yts", [128, 33, 512], BF16), Yt[1][:], [t_Yt], [])
                                S.barrier()
                        if os.environ.get("KE") == "c":
                            pho.close()
                            break
                        with ExitStack() as ph2:
                            psh = PSH(ph2)
                            ub = [SB(ph2, "ub%d" % i, [128, 4, 256]) for i in range(2)]; t_ub = [T(), T()]
                            xg = [SB(ph2, "xg%d" % i, [128, 4, 256]) for i in range(2)]; t_xg = [T(), T()]
                            tt1 = SB(ph2, "tt1", [128, 256]); t_tt1 = T()
                            ysb = SB(ph2, "ysb", [128, 4, 256]); t_ysb = [T() for _ in range(4)]
                            res = [SB(ph2, "res%d" % i, [128, 4, 256]) for i in range(2)]; t_res = [T(), T()]
                            rbf = [SB(ph2, "rbf%d" % i, [128, 4, 256], BF16) for i in range(2)]; t_rbf = [T(), T()]
                            for tb in range(ntb):
                                b = tb % 2
                                c0 = tok0 + tb * 256
                                usrc = Hc[0:512, c0:c0 + 256] if o == 0 else Y1[:, tb * 256:(tb + 1) * 256]
                                S.dma(ub[b][:], usrc.rearrange("(c p) t -> p c t", p=128), (), [t_ub[b]])
                                S.dma(xg[b][:], Hc[(1 + o) * 512:(2 + o) * 512, c0:c0 + 256].rearrange("(c p) t -> p c t", p=128), (), [t_xg[b]])
                                for (r0, nr) in pieces(KCH):
                                    tb_, t_tb = load_tab(r0, nr, tb * 256)
                                    for kcl in range(nr):
                                        kc = r0 + kcl
                                        for cc in range(4):
                                            for si in range(2):
                                                M(lambda kc=kc, kcl=kcl, cc=cc, si=si, tb_=tb_: nc.tensor.matmul(
                                                    psb[cc][:, 0:256], Yt[si][:, kc, cc * 128:(cc + 1) * 128], tb_[:, si, kcl, :],
                                                    start=(kc == 0 and si == 0), stop=(kc == KCH - 1 and si == 1)),
                                                  [t_Yt, t_tb], [tps[cc]], inc=(kcl == nr - 1 and cc == 3 and si == 1))
                                if os.environ.get("KE") == "d":
                                    continue
                                for cc in range(4):
                                    A(lambda cc=cc: nc.scalar.copy(ysb[:, cc, :], psb[cc][:, 0:256]), [tps[cc], t_ysb[cc]], [t_ysb[cc]])
                                for cc in range(4):
                                    V(lambda cc=cc, b=b: nc.vector.scalar_tensor_tensor(tt1[:], ub[b][:, cc, :], spc(l, "hskip", o * 4 + cc, 1), ysb[:, cc, :],
                                                                                        ALU.mult, ALU.add), [t_ub[b], t_ysb[cc], t_sp, t_tt1], [t_tt1])
                                    if o == 0:
                                        V(lambda cc=cc, b=b: nc.vector.tensor_mul(res[b][:, cc, :], tt1[:], xg[b][:, cc, :]), [t_tt1, t_xg[b], t_res[b]], [t_res[b]])
                                        P(lambda cc=cc, b=b: nc.gpsimd.tensor_copy(rbf[b][:, cc, :], res[b][:, cc, :]), [t_res[b], t_rbf[b]], [t_rbf[b]])
                                    else:
                                        V(lambda cc=cc, b=b: nc.vector.tensor_mul(rbf[b][:, cc, :], tt1[:], xg[b][:, cc, :]), [t_tt1, t_xg[b], t_rbf[b]], [t_rbf[b]])
                                if o == 0:
                                    S.dma(Y1[:, tb * 256:(tb + 1) * 256].rearrange("(c p) t -> p c t", p=128), res[b][:], [t_res[b]], [], eng="act")
                                    if os.environ.get("KE") == "e":
                                        continue
                                    for ts_ in range(2):
                                        tc = tb * 2 + ts_
                                        for cc in range(4):
                                            M(lambda cc=cc, ts_=ts_, b=b: nc.tensor.transpose(psh[:, ts_ * 512 + cc * 128:ts_ * 512 + (cc + 1) * 128],
                                                                                            rbf[b][:, cc, ts_ * 128:(ts_ + 1) * 128], ident_bf[:]),
                                              [t_rbf[b], t_ident, tpsh[ts_]], [tpsh[ts_]], inc=(cc == 3))
                                        evac(ts_, Dt[:, tc, :], psh[:, ts_ * 512:(ts_ + 1) * 512], [tpsh[ts_], t_Dt], [t_Dt])
                                else:
                                    S.dma(Yd[2][:, c0:c0 + 256].rearrange("(c p) t -> p c t", p=128), rbf[b][:], [t_rbf[b]], [], eng="act")
                            S.barrier()
                            if "H3" in dbg and l == 0 and n == SEQ and o == 0:
                                S.dma(ddbg("Y1", [512, SEQ]), Y1, [], [])
                                S.barrier()
                        pho.close()
                    S.barrier()
            if stop == "E":
                break
            LT = TILES[1:] if last else TILES
            with ExitStack() as ph:
                stg = SB(ph, "gstg", [128, 1024]); t_stg = T()
                wbr = [load_w_bf16(ph, "wbr%d" % i, w_br[i][l], 4, 1024, stg, t_stg) for i in range(3)]
                wo_bf, t_wo_bf = load_w_bf16(ph, "wo_bf", w_out[l], 8, 1024, stg, t_stg)
                yt = [[SB(ph, "gy%d_%d" % (i, j), [128, 4, 512], BF16) for j in range(2)] for i in range(3)]
                t_yt = [[T(), T()] for _ in range(3)]
                gt = [SB(ph, "gg%d" % j, [128, 24, 512], BF16) for j in range(2)]; t_gt = [T(), T()]
                xt = [SB(ph, "gx%d" % j, [128, 8, 512]) for j in range(2)]; t_xt = [T(), T()]
                xo = [SB(ph, "gxo%d" % j, [128, 8, 512]) for j in range(2)]; t_xo = [T(), T()]
                mg = SB(ph, "gmg", [128, 8, 512], BF16); t_mg = T()
                m1 = SB(ph, "gm1", [128, 512]); m2_ = SB(ph, "gm2", [128, 512]); m3 = SB(ph, "gm3", [128, 512]); t_m = [T(), T(), T()]
                for ti, (t0, n) in enumerate(LT):
                    b = ti % 2
                    ty = 1 if t0 < CTX else 0
                    for i in range(3):
                        S.dma(yt[i][b][:, :, 0:n], Yd[i][:, t0:t0 + n].rearrange("(k p) t -> p k t", p=128), (), [t_yt[i][b]])
                    S.dma(gt[b][:, :, 0:n], Gd[:, t0:t0 + n].rearrange("(k p) t -> p k t", p=128), (), [t_gt[b]])
                    S.dma(xt[b][:, :, 0:n], Xsrc[:, t0:t0 + n].rearrange("(k p) t -> p k t", p=128), (), [t_xt[b]])
                    for m in range(8):
                        for i in range(3):
                            for k in range(4):
                                M(lambda i=i, k=k, m=m, n=n, b=b: nc.tensor.matmul(psb[i][:, 0:n], wbr[i][0][:, k, m * 128:(m + 1) * 128], yt[i][b][:, k, 0:n],
                                                                                 start=(k == 0), stop=(k == 3)), [wbr[i][1], t_yt[i][b]], [tps[i]], inc=(k == 3))
                        V(lambda m=m, n=n, b=b: nc.vector.tensor_mul(m1[:, 0:n], psb[0][:, 0:n], gt[b][:, m, 0:n]), [tps[0], t_gt[b], t_m[0]], [t_m[0]])
                        V(lambda m=m, n=n, b=b: nc.vector.tensor_mul(m2_[:, 0:n], psb[1][:, 0:n], gt[b][:, 8 + m, 0:n]), [tps[1], t_gt[b], t_m[1]], [t_m[1]])
                        V(lambda m=m, n=n, b=b: nc.vector.tensor_mul(m3[:, 0:n], psb[2][:, 0:n], gt[b][:, 16 + m, 0:n]), [tps[2], t_gt[b], t_m[2]], [t_m[2]])
                        P(lambda n=n: nc.gpsimd.tensor_add(m1[:, 0:n], m1[:, 0:n], m2_[:, 0:n]), [t_m[0], t_m[1]], [t_m[0]])
                        P(lambda m=m, n=n: nc.gpsimd.tensor_add(mg[:, m, 0:n], m1[:, 0:n], m3[:, 0:n]), [t_m[0], t_m[2], t_mg], [t_mg])
                    for m in range(8):
                        pb = 3 + (m % 3)
                        for k in range(8):
                            M(lambda k=k, m=m, n=n, pb=pb: nc.tensor.matmul(psb[pb][:, 0:n], wo_bf[:, k, m * 128:(m + 1) * 128], mg[:, k, 0:n],
                                                                            start=(k == 0), stop=(k == 7)), [t_wo_bf, t_mg], [tps[pb]], inc=(k == 7))
                        V(lambda m=m, n=n, b=b, pb=pb, ty=ty: nc.vector.scalar_tensor_tensor(
                            xo[b][:, m, 0:n], psb[pb][:, 0:n], mods[:, l, 16 + m, ty:ty + 1], xt[b][:, m, 0:n], ALU.mult, ALU.add),
                          [tps[pb], t_xt[b], t_mods, t_xo[b]], [t_xo[b]])
                    S.dma(Xd[:, t0:t0 + n].rearrange("(k p) t -> p k t", p=128), xo[b][:, :, 0:n], [t_xo[b]], [], eng="act")
                S.barrier()
            if stop == "G":
                break
            with ExitStack() as ph:
                xn2 = SB(ph, "xn2", [128, 8, TT], BF16); t_xn2 = T()
                with ExitStack() as ph2:
                    norm_mod(ph2, l, Xd, tX, A1[:, l, 1], mods[:, l, 24:32, :], xn2, t_xn2, LT, "h")
                    S.barrier()
                with ExitStack() as ph2:
                    wst2 = [SB(ph2, "hws%d" % i, [128, 2, 8, 128]) for i in range(2)]; t_wst2 = [T(), T()]
                    wb2 = [SB(ph2, "hwb%d" % i, [128, 2, 8, 128], BF16) for i in range(2)]; t_wb2 = [T(), T()]
                    sg = [SB(ph2, "hsg%d" % i, [128, 512]) for i in range(2)]; t_sg = [T(), T()]
                    ao = [SB(ph2, "hao%d" % i, [128, TT], BF16) for i in range(2)]; t_ao = [T(), T()]
                    pi = 0
                    for hc in range(22):
                        b = hc % 2
                        for wi, wsrc in enumerate((w_fg, w_fu)):
                            S.dma(wst2[b][:, wi], wsrc[l, :, hc * 128:(hc + 1) * 128].rearrange("(k p) n -> p k n", p=128), (), [t_wst2[b]])
                        P(lambda b=b: nc.gpsimd.tensor_copy(wb2[b][:], wst2[b][:]), [t_wst2[b]], [t_wb2[b]])
                        for ti, (t0, n) in enumerate(LT):
                            pg = (pi % 3) * 2
                            pi += 1
                            for wi in range(2):
                                for k in range(8):
                                    M(lambda b=b, wi=wi, k=k, t0=t0, n=n, pg=pg: nc.tensor.matmul(
                                        psb[pg + wi][:, 0:n], wb2[b][:, wi, k, :], xn2[:, k, t0:t0 + n], start=(k == 0), stop=(k == 7)),
                                      [t_wb2[b], t_xn2], [tps[pg + wi]], inc=(k == 7))
                            sb_ = pi % 2
                            A(lambda n=n, pg=pg, sb_=sb_: nc.scalar.activation(sg[sb_][:, 0:n], psb[pg][:, 0:n], AF.Silu), [tps[pg], t_sg[sb_]], [t_sg[sb_]])
                            V(lambda n=n, pg=pg, sb_=sb_, b=b, t0=t0: nc.vector.tensor_mul(ao[b][:, t0:t0 + n], sg[sb_][:, 0:n], psb[pg + 1][:, 0:n]),
                              [t_sg[sb_], tps[pg + 1], t_ao[b]], [t_ao[b]])
                        S.dma(ACTd[hc * 128:(hc + 1) * 128, :], ao[b][:], [t_ao[b]], [], eng="act")
                    S.barrier()
            with ExitStack() as ph:
                stg = SB(ph, "dstg", [128, 1024]); t_stg = T()
                wd_bf, t_wd = load_w_bf16(ph, "wd_bf", w_fd[l], 22, 1024, stg, t_stg)
                at = [SB(ph, "dat%d" % j, [128, 22, 512], BF16) for j in range(2)]; t_at = [T(), T()]
                xt = [SB(ph, "dx%d" % j, [128, 8, 512]) for j in range(2)]; t_xt = [T(), T()]
                xo = [SB(ph, "dxo%d" % j, [128, 8, 512]) for j in range(2)]; t_xo = [T(), T()]
                for ti, (t0, n) in enumerate(LT):
                    b = ti % 2
                    ty = 1 if t0 < CTX else 0
                    S.dma(at[b][:, :, 0:n], ACTd[:, t0:t0 + n].rearrange("(k p) t -> p k t", p=128), (), [t_at[b]])
                    S.dma(xt[b][:, :, 0:n], Xd[:, t0:t0 + n].rearrange("(k p) t -> p k t", p=128), (), [t_xt[b]])
                    for m in range(8):
                        pb = m % 4
                        for k in range(22):
                            M(lambda k=k, m=m, n=n, pb=pb, b=b: nc.tensor.matmul(psb[pb][:, 0:n], wd_bf[:, k, m * 128:(m + 1) * 128], at[b][:, k, 0:n],
                                                                                 start=(k == 0), stop=(k == 21)), [t_wd, t_at[b]], [tps[pb]], inc=(k == 21))
                        V(lambda m=m, n=n, b=b, pb=pb, ty=ty: nc.vector.scalar_tensor_tensor(
                            xo[b][:, m, 0:n], psb[pb][:, 0:n], mods[:, l, 40 + m, ty:ty + 1], xt[b][:, m, 0:n], ALU.mult, ALU.add),
                          [tps[pb], t_xt[b], t_mods, t_xo[b]], [t_xo[b]])
                    S.dma(Xd[:, t0:t0 + n].rearrange("(k p) t -> p k t", p=128), xo[b][:, :, 0:n], [t_xo[b]], [], eng="act")
                S.barrier()
            if stop == "H":
                break

        if stop is None:
            with ExitStack() as ph:
                xt = [SB(ph, "fx%d" % i, [128, 8, 512]) for i in range(2)]; t_xt = [T(), T()]
                sq = SB(ph, "fsq2", [128, 8, 512], BF16); t_sq = T()
                rstd = SB(ph, "frstd", [128, 512]); t_rstd = T()
                ot = [SB(ph, "fo%d" % i, [128, 8, 512]) for i in range(2)]; t_ot = [T(), T()]
                for ti, (t0, n) in enumerate(TILES[1:]):
                    b = ti % 2
                    S.dma(xt[b][:], Xd[:, t0:t0 + n].rearrange("(k p) t -> p k t", p=128), (), [t_xt[b]])
                    A(lambda b=b: nc.scalar.activation(sq[:], xt[b][:], AF.Square), [t_xt[b], t_sq], [t_sq])
                    for k in range(8):
                        M(lambda k=k: nc.tensor.matmul(psb[1][:, :], ones_bf[:], sq[:, k, :], start=(k == 0), stop=(k == 7)),
                          [t_ones, t_sq], [tps[1]], inc=(k == 7))
                    V(lambda: nc.vector.tensor_scalar(rstd[:], psb[1][:, :], 1.0 / D, EPS, ALU.mult, ALU.add), [tps[1], t_rstd], [t_rstd])
                    A(lambda: nc.scalar.activation(rstd[:], rstd[:], AF.Sqrt), [t_rstd], [t_rstd])
                    V(lambda: nc.vector.reciprocal(rstd[:], rstd[:]), [t_rstd], [t_rstd])
                    for k in range(8):
                        V(lambda k=k, b=b: nc.vector.scalar_tensor_tensor(ot[b][:, k, :], xt[b][:, k, :], spc(0, "fng", k, 1), rstd[:], ALU.mult, ALU.mult),
                          [t_xt[b], t_rstd, t_sp, t_ot[b]], [t_ot[b]])
                    S.dma(outT[:, t0 - CTX:t0 - CTX + n].rearrange("(k p) t -> p k t", p=128), ot[b][:], [t_ot[b]], [], eng="act")
                S.barrier()

        for name in ("Zd", "Gd", "Ya", "Yb", "Yc", "Xd"):
            if name in dbg:
                src = {"Zd": Zd, "Gd": Gd, "Ya": Yd[0], "Yb": Yd[1], "Yc": Yd[2], "Xd": Xd}[name]
                dd = ddbg(name, src.shape, src.dtype if hasattr(src, "dtype") else F32)
                S.dma(dd, src, [], [], eng="sp")
        S.finish()
    return nc, dbg_out


def shared_maps(inp):
    c = host_consts()
    f = lambda k: np.ascontiguousarray(np.asarray(inp[k], np.float32))
    m = {}
    m["sp"] = np.stack([pack_small(inp, l) for l in range(NL)])
    for k in ("ada_w", "w_in", "w_uq", "w_ukv", "hy_w1", "hy_w2", "hy_w_out", "w_br_a", "w_br_b", "w_br_c", "w_out",
              "ffn_w_gate", "ffn_w_up", "ffn_w_down"):
        m[k] = f(k)
    perm = np.concatenate([np.arange(8, 16), np.arange(0, 8), np.arange(24, 32), np.arange(16, 24)])
    m["w_kperot"] = np.ascontiguousarray(m["w_in"][:, :, 640:672][:, :, perm])
    wq = m["w_uq"].reshape(NL, 384, 8, 96)
    wqr = wq.copy()
    wqr[..., 64:96] = wq[..., 64:96][..., perm]
    m["w_uqrot"] = np.ascontiguousarray(wqr.reshape(NL, 384, 768))
    bd = np.zeros((NL, 2, 2, 4, 128, 128), np.float32)
    for wi, key in enumerate(("lru_wa", "lru_wx")):
        w = np.asarray(inp[key], np.float32)
        for j in range(4):
            bd[:, :, wi, j, 0:64, 0:64] = w[:, :, 2 * j]
            bd[:, :, wi, j, 64:128, 64:128] = w[:, :, 2 * j + 1]
    m["lru_bd"] = bd
    for k in ("rope", "ctab", "stab", "ctab2", "stab2", "wk", "wk2", "zemb", "zemb2", "decay", "decay2"):
        m[k] = c[k]
    return m


def core_map(inp, b, shared):
    m = dict(shared)
    x = np.asarray(inp["x"][b], np.float32)
    ctx = np.asarray(inp["ctx"][b], np.float32)
    m["xT"] = np.ascontiguousarray(np.concatenate([ctx, x], axis=0).T)
    cnd = np.stack([np.asarray(inp["c"][b], np.float32), np.asarray(inp["c_ctx"], np.float32)], axis=1)
    m["cond"] = np.ascontiguousarray(cnd.reshape(8, 128, 2).transpose(1, 0, 2).reshape(128, 16))
    return m


_NC = {}


def kernel(**inputs):
    if "nc" not in _NC:
        _NC["nc"] = build()[0]
    nc = _NC["nc"]
    shared = shared_maps(inputs)
    in_maps = [core_map(inputs, i // 2, shared) for i in range(8)]
    res = run_bass_kernel_spmd(nc, in_maps, core_ids=list(range(8)))
    out = np.stack([np.ascontiguousarray(res.results[2 * b]["outT"].T) for b in range(4)], axis=0)
    return out.astype(np.float32)
```
